# Optimizing a Trainium2 kernel written in Bass

```python
import math
import jax, jax.numpy as jnp
from jax import lax
import numpy as np

D_MODEL = 1024
BATCH = 8
SEQ = 2048
DEPTH = 2

CHUNK = 64
N_MIXERS = 2
N_A = (DEPTH + 1) // 2
N_B = DEPTH // 2
GMLP_BLOCK = 128
D_GATE = 2 * D_MODEL
A_GROUPS = 8
A_GROUP_DIM = D_GATE // A_GROUPS
CONV_WIDTH = 3
D_FF = 4 * D_MODEL
LN_EPS = 1e-5
DEEPNORM_ALPHA = (2.0 * DEPTH) ** 0.25
DEEPNORM_BETA = (8.0 * DEPTH) ** -0.25

kernel_name = "hybrid_sgu_shortconv_deepnorm_trunk"


def layer_norm(x, g, b):
    xf = x.astype(jnp.float32)
    mu = jnp.mean(xf, axis=-1, keepdims=True)
    var = jnp.mean(jnp.square(xf - mu), axis=-1, keepdims=True)
    y = (xf - mu) * lax.rsqrt(var + LN_EPS) * g.astype(jnp.float32) + b.astype(jnp.float32)
    return y.astype(x.dtype)


def spatial_gating_mixer(x, w_in, b_in, v_g, v_b, w_s, b_s, w_out, b_out):
    bsz, seq, _ = x.shape
    h = jax.nn.gelu(x @ w_in + b_in, approximate=False)
    u, v = h[..., :D_GATE], h[..., D_GATE:]
    v = layer_norm(v, v_g, v_b)
    v = v.reshape(bsz, seq // GMLP_BLOCK, GMLP_BLOCK, A_GROUPS, A_GROUP_DIM)
    chunk_id = jnp.arange(GMLP_BLOCK) // CHUNK
    mask = chunk_id[:, None] >= chunk_id[None, :]
    w = jnp.where(mask[None], w_s, jnp.zeros_like(w_s))
    v = jnp.einsum('gts,bnsgc->bntgc', w, v) + b_s.T[None, None, :, :, None]
    y = u * v.reshape(bsz, seq, D_GATE)
    return y @ w_out + b_out


def short_conv_mixer(x, w_in, conv_w, w_out):
    bch = x @ w_in
    b_gate = bch[..., :D_MODEL]
    c_gate = bch[..., D_MODEL:2 * D_MODEL]
    h = bch[..., 2 * D_MODEL:]
    h = c_gate * h
    h = lax.conv_general_dilated(
        h, conv_w[:, None, :].astype(h.dtype),
        window_strides=(1,), padding=[(CONV_WIDTH - 1, 0)],
        dimension_numbers=('NWC', 'WIO', 'NWC'),
        feature_group_count=D_MODEL)
    return (b_gate * h) @ w_out


def squared_relu_mlp(x, w1, w2):
    return jnp.square(jax.nn.relu(x @ w1)) @ w2


def setup_inputs(seed: int = 0) -> dict:
    key = jax.random.key(seed)
    ks = jax.random.split(key, 20)
    f32 = jnp.float32
    nrm = lambda k, shape, scale: jax.random.normal(k, shape, f32) * scale
    x = jax.random.normal(ks[0], (BATCH, SEQ, D_MODEL), f32)
    ln_g = 1.0 + nrm(ks[1], (DEPTH, 2, D_MODEL), 0.02)
    ln_b = nrm(ks[2], (DEPTH, 2, D_MODEL), 0.02)
    a_w_in = nrm(ks[3], (N_A, D_MODEL, 2 * D_GATE), D_MODEL ** -0.5)
    a_b_in = nrm(ks[4], (N_A, 2 * D_GATE), 0.02)
    a_v_g = 1.0 + nrm(ks[5], (N_A, D_GATE), 0.02)
    a_v_b = nrm(ks[6], (N_A, D_GATE), 0.02)
    a_w_s = nrm(ks[7], (N_A, A_GROUPS, GMLP_BLOCK, GMLP_BLOCK), GMLP_BLOCK ** -0.5)
    a_b_s = 1.0 + nrm(ks[8], (N_A, A_GROUPS, GMLP_BLOCK), 0.02)
    a_w_out = nrm(ks[9], (N_A, D_GATE, D_MODEL), DEEPNORM_BETA * D_GATE ** -0.5)
    a_b_out = nrm(ks[10], (N_A, D_MODEL), 0.02)
    b_w_in = nrm(ks[11], (N_B, D_MODEL, 3 * D_MODEL), D_MODEL ** -0.5)
    b_conv = nrm(ks[12], (N_B, CONV_WIDTH, D_MODEL), CONV_WIDTH ** -0.5)
    b_w_out = nrm(ks[13], (N_B, D_MODEL, D_MODEL), DEEPNORM_BETA * D_MODEL ** -0.5)
    mlp_w1 = nrm(ks[14], (DEPTH, D_MODEL, D_FF), D_MODEL ** -0.5)
    mlp_w2 = nrm(ks[15], (DEPTH, D_FF, D_MODEL), DEEPNORM_BETA * D_FF ** -0.5)
    return {"x": x, "ln_g": ln_g, "ln_b": ln_b,
            "a_w_in": a_w_in, "a_b_in": a_b_in, "a_v_g": a_v_g, "a_v_b": a_v_b,
            "a_w_s": a_w_s, "a_b_s": a_b_s, "a_w_out": a_w_out, "a_b_out": a_b_out,
            "b_w_in": b_w_in, "b_conv": b_conv, "b_w_out": b_w_out,
            "mlp_w1": mlp_w1, "mlp_w2": mlp_w2}


def reference(x, ln_g, ln_b, a_w_in, a_b_in, a_v_g, a_v_b, a_w_s, a_b_s, a_w_out,
              a_b_out, b_w_in, b_conv, b_w_out, mlp_w1, mlp_w2):
    alpha = jnp.asarray(DEEPNORM_ALPHA, x.dtype)
    for i in range(DEPTH):
        j = i // N_MIXERS
        if i % N_MIXERS == 0:
            mix = spatial_gating_mixer(x, a_w_in[j], a_b_in[j], a_v_g[j], a_v_b[j],
                                       a_w_s[j], a_b_s[j], a_w_out[j], a_b_out[j])
        else:
            mix = short_conv_mixer(x, b_w_in[j], b_conv[j], b_w_out[j])
        x = layer_norm(alpha * x + mix, ln_g[i, 0], ln_b[i, 0])
        x = layer_norm(alpha * x + squared_relu_mlp(x, mlp_w1[i], mlp_w2[i]),
                       ln_g[i, 1], ln_b[i, 1])
    return x
```

```python
from contextlib import ExitStack
import numpy as np
import concourse.bass as bass
import concourse.mybir as mybir
from concourse.bass_utils import run_bass_kernel_spmd

F32 = mybir.dt.float32
BF16 = mybir.dt.bfloat16
AF = mybir.ActivationFunctionType
ALU = mybir.AluOpType

D = 1024
SEQ = 2048
NB = 8
TH = 1024
NT = TH // 128
ALPHA = float((2.0 * 2) ** 0.25)
EPS = 1e-5
ENGINES = ("pe", "act", "dve", "pool", "sp")


ALIAS = {
    ("scrB", 0): (("B", 0, 0), ("B", 0, 1)), ("scrB", 1): (("B", 1, 0), ("B", 1, 1)),
    ("hib", 0): (("B", 0, 0),), ("hib", 1): (("B", 0, 1),),
    "hi32": (("B", 1, 0), ("B", 1, 1)),
    ("csb", 0): (("B", 0, 0),), ("csb", 1): (("B", 0, 1),),
    ("acc", 0): (("B", 1, 0),), ("acc", 1): (("B", 1, 1),),
    "b32a": (("A", 0), ("A", 1)), "b32b": (("A", 2), ("A", 3)),
    ("lo32", 0): (("A", 0), ("A", 1)), ("lo32", 1): (("A", 2), ("A", 3)),
    ("scrA0", 0): (("A", 0),), ("scrA0", 1): (("A", 1),), ("scrA0", 2): (("A", 2),), ("scrA0", 3): (("A", 3),),
    ("hc", 0): (("A", 0),), ("hc", 1): (("A", 0), ("A", 1)), ("hc", 2): (("A", 1), ("A", 2)),
}
class Op:
    __slots__ = ("eng", "fn", "reads", "writes", "is_dma", "stream", "idx", "eidx",
                 "waits", "signal", "sig_val", "dma_val")

    def __init__(self, eng, fn, reads, writes, is_dma, stream):
        self.eng = eng
        self.fn = fn
        self.reads = reads
        self.writes = writes
        self.is_dma = is_dma
        self.stream = stream
        self.waits = []
        self.signal = False
        self.sig_val = None
        self.dma_val = None


class Prog:
    def __init__(self):
        self.ops = []
        self.eng_ops = {e: [] for e in ENGINES}
        self.last_writer = {}
        self.readers = {}
        self.dma_streams = {}
        self.waited = {e: {} for e in ENGINES}
        self.bank_busy = {}

    def _expand(self, keys):
        out = []
        for k in keys:
            out.extend(ALIAS.get(k, (k,)))
        return tuple(out)

    def add(self, eng, fn, reads=(), writes=(), is_dma=False, stream=None):
        op = Op(eng, fn, self._expand(reads), self._expand(writes), is_dma, stream)
        for k in op.writes:
            if isinstance(k, tuple) and k[0] == "ps" and eng == "pe":
                assert not self.bank_busy.get(k[1], False), ("PSUM bank reused before its evacuation was emitted", k)
                self.bank_busy[k[1]] = True
        for k in op.reads:
            if isinstance(k, tuple) and k[0] == "ps" and eng != "pe":
                self.bank_busy[k[1]] = False
        op.idx = len(self.ops)
        op.eidx = len(self.eng_ops[eng])
        self.ops.append(op)
        self.eng_ops[eng].append(op)
        if is_dma:
            c = self.dma_streams.get(stream, 0) + 1
            self.dma_streams[stream] = c
            op.dma_val = 16 * c
        deps = {}

        def need(src, raw):
            if src is None or src is op:
                return
            prev = deps.get(id(src))
            deps[id(src)] = (src, raw if prev is None else (prev[1] or raw))

        for k in op.reads:
            need(self.last_writer.get(k), True)
        for k in op.writes:
            need(self.last_writer.get(k), False)
            rd = self.readers.get(k)
            if rd:
                for r in rd.values():
                    need(r, False)
        best = {}
        for src, raw in deps.values():
            if src.is_dma:
                key = ("dma", src.stream)
                if src.dma_val > self.waited[eng].get(key, 0):
                    self.waited[eng][key] = src.dma_val
                    op.waits.append(("dma", src.stream, src.dma_val))
                continue
            if src.eng == eng and not op.is_dma:
                if eng == "pe" or not raw:
                    continue
            cur = best.get(src.eng)
            if cur is None or src.eidx > cur.eidx:
                best[src.eng] = src
        for se, src in best.items():
            key = ("eng", se)
            if src.eidx > self.waited[eng].get(key, -1):
                self.waited[eng][key] = src.eidx
                src.signal = True
                op.waits.append(("eng", src, None))
        for k in op.writes:
            self.last_writer[k] = op
            self.readers[k] = {}
        for k in op.reads:
            d = self.readers.setdefault(k, {})
            d[eng if not is_dma else ("dma", op.idx)] = op
        return op

    def finalize(self):
        for e in ENGINES:
            c = 0
            for op in self.eng_ops[e]:
                if op.signal:
                    assert not op.is_dma
                    c += 1
                    op.sig_val = c

    def emit_engine(self, name, eng, sem_eng, sem_dma):
        for op in self.eng_ops[name]:
            for kind, a, b in op.waits:
                if kind == "dma":
                    eng.wait_ge(sem_dma[a], b)
                else:
                    eng.wait_ge(sem_eng[a.eng], a.sig_val)
            ins = op.fn(eng)
            if op.is_dma:
                ins.then_inc(sem_dma[op.stream], 16)
            elif op.signal:
                ins.then_inc(sem_eng[name], 1)


def _fblocks(w):
    K, N = w.shape
    nf = N // 128
    return np.ascontiguousarray(w.reshape(K // 128, 128, nf, 128).transpose(2, 1, 0, 3).reshape(nf, 128, (K // 128) * 128))


def _tblock(w, r0, nr, c0):
    nk = nr // 128
    return w[r0:r0 + nr, c0:c0 + 512].reshape(nk, 128, 512).transpose(1, 0, 2).reshape(128, nk * 512)


def prepare_weights(inp):
    f = lambda a: np.asarray(a, dtype=np.float32)
    a_w_in = f(inp["a_w_in"])[0]
    a_w_out = f(inp["a_w_out"])[0]
    b_w_in = f(inp["b_w_in"])[0]
    b_w_out = f(inp["b_w_out"])[0]
    w1 = f(inp["mlp_w1"])
    w2 = f(inp["mlp_w2"])
    wsm = np.concatenate([_fblocks(a_w_in[:, :2048]), _fblocks(b_w_in), _fblocks(w1[0]), _fblocks(w1[1])], axis=0)
    big = []
    for i in range(2):
        big.append(np.concatenate([_tblock(a_w_in, 0, 1024, 2048 + (2 * i) * 512),
                                   _tblock(a_w_in, 0, 1024, 2048 + (2 * i + 1) * 512)], axis=1))
    for cb in range(2):
        big.append(_tblock(a_w_out, 0, 2048, cb * 512))
    for l in range(2):
        for fh in range(2):
            for cb in range(2):
                big.append(_tblock(w2[l], fh * 2048, 2048, cb * 512))
    big.append(np.concatenate([_tblock(b_w_out, 0, 1024, 0), _tblock(b_w_out, 0, 1024, 512)], axis=1))
    wbig = np.ascontiguousarray(np.stack(big, axis=0))
    cols = np.zeros((128, 73), np.float32)
    cols[0, 72] = 1.0
    b_in = f(inp["a_b_in"])[0]
    cols[:, 0:16] = b_in[:2048].reshape(16, 128).T
    cols[:, 16:32] = f(inp["a_v_g"])[0].reshape(16, 128).T
    cols[:, 32:48] = f(inp["a_v_b"])[0].reshape(16, 128).T
    conv = f(inp["b_conv"])[0]
    for k in range(3):
        cols[:, 48 + k * 8:48 + (k + 1) * 8] = conv[k].reshape(8, 128).T
    rows = np.zeros((12, 1024), np.float32)
    ln_g = f(inp["ln_g"]); ln_b = f(inp["ln_b"])
    for l in range(2):
        for s in range(2):
            rows[l * 2 + s] = ln_g[l, s]
            rows[4 + l * 2 + s] = ln_b[l, s]
    rows[8] = f(inp["a_b_out"])[0]
    rows[9:11] = b_in[2048:].reshape(2, 1024)
    rows[11] = f(inp["a_b_s"])[0].reshape(1024)
    wsT = np.ascontiguousarray(f(inp["a_w_s"])[0].transpose(2, 0, 1).reshape(128, 1024))
    return {"wsm": wsm, "wbig": wbig, "cols": cols, "rows": rows, "wsT": wsT}


SM_BASE = {"a_u": 0, "b_in": 16, "w1_0": 40, "w1_1": 72}
BIG_BASE = {"a_v": 0, "a_out": 2, "w2_0": 4, "w2_1": 8, "b_out": 12}


NCOLS = 73
NFILL = 4


def build_nc(layers=(0, 1)):
    nc = bass.Bass("TRN2", target_bir_lowering=False)
    x_d = nc.dram_tensor("x", [SEQ, D], F32, kind="ExternalInput")
    wsm_d = nc.dram_tensor("wsm", [104, 128, 1024], F32, kind="ExternalInput")
    wbig_d = nc.dram_tensor("wbig", [13, 128, 8192], F32, kind="ExternalInput")
    cols_d = nc.dram_tensor("cols", [128, NCOLS], F32, kind="ExternalInput")
    rows_d = nc.dram_tensor("rows", [12, 1024], F32, kind="ExternalInput")
    wsT_d = nc.dram_tensor("wsT", [128, 1024], F32, kind="ExternalInput")
    out_d = nc.dram_tensor("out", [SEQ, D], F32, kind="ExternalOutput")
    x_ap = x_d.ap()
    out_ap = out_d.ap()
    wsm_ap = wsm_d.ap()
    wbig_ap = wbig_d.ap()

    def row_bc(r0, n, parts=128):
        return bass.AP(rows_d, r0, [[0, parts], [1, n]])

    es = ExitStack()
    with es:
        def sb(name, shape, dt):
            return es.enter_context(nc.sbuf_tensor(name, shape, dt))

        xres = sb("xres", [128, NT, D], F32)
        xT = sb("xT", [128, 8, TH], BF16)
        H = sb("H", [128, 16, TH], BF16)
        wbig = [sb(f"wbig{i}", [128, 16, 512], BF16) for i in range(4)]
        wsm = [sb(f"wsm{i}", [128, 8, 128], BF16) for i in range(4)]
        scrA = [sb(f"scrA{i}", [128, 2050], F32) for i in range(1)]
        scrB = [sb(f"scrB{i}", [128, 1024], F32) for i in range(2)]
        t1 = [sb(f"t1_{i}", [128, 512], F32) for i in range(2)]
        xb = [sb(f"xb{i}", [128, D], BF16) for i in range(3)]
        lnc = sb("lnc", [128, 2, D], F32)
        bias2 = sb("bias2", [128, 16, 128], F32)
        bhl = sb("bhl", [2, 2048], BF16)
        wsT = sb("wsT_sb", [128, 8, 128], BF16)
        ones_bf = sb("ones_bf", [128, 128], BF16)
        ident = sb("ident", [128, 128], BF16)
        cols = sb("cols_sb", [128, NCOLS], F32)
        convst = sb("convst", [128, 8, 2], F32)
        st = [sb(f"st{i}", [128, 4, 6], F32) for i in range(2)]
        mv = [sb(f"mv{i}", [128, 2], F32) for i in range(2)]
        rs = [sb(f"rs{i}", [128, 1], F32) for i in range(2)]
        nm = [sb(f"nm{i}", [128, 1], F32) for i in range(2)]
        psum = [es.enter_context(nc.psum_tensor(f"ps{i}", [128, 512], F32)) for i in range(8)]
        psum_bf = [p.bitcast(BF16) for p in psum]
        vn = [s_.bitcast(BF16) for s_ in scrB]

        P = Prog()
        state = {"bank": 0, "sm_issue": 0, "sm_use": 0, "big_issue": 0, "big_use": 0, "t1": 0,
                 "xb": 0, "stat": 0, "vslot": 0, "ost": 0}
        big_pending = []
        pend_T = []

        def next_bank():
            b = state["bank"] % 8
            state["bank"] += 1
            return b

        sm_plan, big_plan = [], []
        for h in range(2):
            for l in layers:
                if l == 0:
                    sm_plan += [SM_BASE["a_u"] + i for i in range(16)]
                    big_plan += [BIG_BASE["a_v"], BIG_BASE["a_v"] + 1, BIG_BASE["a_out"], BIG_BASE["a_out"] + 1]
                else:
                    for dc in range(8):
                        sm_plan += [SM_BASE["b_in"] + 8 + dc, SM_BASE["b_in"] + 16 + dc, SM_BASE["b_in"] + dc]
                    big_plan += [BIG_BASE["b_out"]]
                sm_plan += [SM_BASE[f"w1_{l}"] + i for i in range(NFILL)]
                for fh in range(2):
                    sm_plan += [SM_BASE[f"w1_{l}"] + fh * 16 + i for i in range(16)]
                    big_plan += [BIG_BASE[f"w2_{l}"] + fh * 2 + cb for cb in range(2)]

        def pump_big(n=1, upto_block=None):
            while big_pending and (n > 0 or (upto_block is not None and big_pending[0][0] <= upto_block)):
                i, slot, q, blk = big_pending.pop(0)
                P.add("pool", lambda e, slot=slot, q=q, blk=blk: e.dma_start(
                    out=wbig[slot][:, 4 * q:4 * q + 4, :].rearrange("p a b -> p (a b)"),
                    in_=wbig_ap[blk][:, q * 2048:(q + 1) * 2048]),
                    writes=[("wb", slot, q)], is_dma=True, stream=("wb", slot, q))
                n -= 1

        def issue_sm(upto):
            while state["sm_issue"] <= upto and state["sm_issue"] < len(sm_plan):
                i = state["sm_issue"]
                slot = i % 4
                blk = sm_plan[i]
                P.add("pool", lambda e, slot=slot, blk=blk: e.dma_start(
                    out=wsm[slot][:].rearrange("p a b -> p (a b)"), in_=wsm_ap[blk]),
                    writes=[("ws", slot)], is_dma=True, stream=("ws", slot))
                state["sm_issue"] += 1
                pump_big(1)

        def issue_big(upto):
            while state["big_issue"] <= upto and state["big_issue"] < len(big_plan):
                i = state["big_issue"]
                for q in range(4):
                    big_pending.append((i, i % 4, q, big_plan[i]))
                state["big_issue"] += 1

        def use_sm(blk, la=3):
            i = state["sm_use"]
            assert sm_plan[i] == blk, (i, sm_plan[i], blk)
            issue_sm(i + la)
            state["sm_use"] += 1
            return i % 4

        def use_big(blk, la=3):
            i = state["big_use"]
            assert big_plan[i] == blk, (i, big_plan[i], blk)
            issue_big(i + la)
            pump_big(0, upto_block=i)
            state["big_use"] += 1
            return i % 4

        P.add("sp", lambda e: e.dma_start(out=cols[:], in_=cols_d.ap()), writes=["cols"], is_dma=True, stream="c0")
        identf = scrB[0]
        P.add("dve", lambda e: e.memset(identf[:, 0:128], 1.0), writes=[("scrB", 0)])
        P.add("pool", lambda e: e.affine_select(out=identf[:, 0:128], in_=identf[:, 0:128], pattern=[[-1, 128]],
                                                  compare_op=ALU.is_equal, fill=0.0, base=0, channel_multiplier=1),
              reads=[("scrB", 0)], writes=[("scrB", 0)])
        P.add("dve", lambda e: e.tensor_copy(ident[:], identf[:, 0:128]), reads=[("scrB", 0)], writes=["ident"])
        def init_layer0():
            P.add("dve", lambda e: e.memset(ones_bf[:], 1.0), writes=["ones"])
            P.add("pool", lambda e: e.dma_start(out=wsT[:].rearrange("p a b -> p (a b)"), in_=wsT_d.ap()),
                  writes=["wsT"], is_dma=True, stream="c1")
            P.add("dve", lambda e: e.memset(wsT[64:128, :, 0:64], 0.0), reads=["wsT"], writes=["wsT"])
            bsbc = lnc[:, 1, :]
            P.add("sp", lambda e: e.dma_start(out=lnc[:, 1, :], in_=row_bc(11 * 1024, 1024)), writes=["lnc_b"],
                  is_dma=True, stream="lncb")
            for g in range(8):
                bk = next_bank()
                P.add("pe", lambda e, g=g, bk=bk: e.matmul(psum[bk][:, 0:128], ones_bf[:], wsT[:, g, :],
                                                           start=True, stop=True),
                      reads=["ones", "wsT"], writes=[("ps", bk)])
                for j in range(2):
                    c = 2 * g + j
                    P.add("dve", lambda e, g=g, c=c, bk=bk: e.scalar_tensor_tensor(
                        out=bias2[:, c, :], in0=psum[bk][:, 0:128], scalar=cols[:, 32 + c:33 + c],
                        in1=lnc[:, 1, g * 128:(g + 1) * 128], op0=ALU.mult, op1=ALU.add),
                        reads=[("ps", bk), "cols", "lnc_b"], writes=[("bias2", c)])

            b32 = scrA[0]
            P.add("sp", lambda e: e.dma_start(out=b32[0:2, 0:1024], in_=row_bc(9 * 1024, 1024, parts=2)),
                  writes=["b32a"], is_dma=True, stream="c3")
            P.add("sp", lambda e: e.dma_start(out=b32[0:2, 1024:2048], in_=row_bc(10 * 1024, 1024, parts=2)),
                  writes=["b32b"], is_dma=True, stream="c5")
            for hf in range(2):
                sl = slice(hf * 1024, (hf + 1) * 1024)
                src = b32[0:2, hf * 1024:(hf + 1) * 1024]
                P.add("dve", lambda e, sl=sl, src=src: e.tensor_copy(vn[0][0:2, sl], src),
                      reads=["b32a", "b32b", "ident"], writes=[("hib", hf), ("scrB", 0)])
                P.add("dve", lambda e, sl=sl: e.tensor_copy(scrB[1][0:2, 0:1024], vn[0][0:2, sl]),
                      reads=[("hib", hf)], writes=["hi32", ("scrB", 1)])
                P.add("dve", lambda e, src=src: e.tensor_tensor(src, src, scrB[1][0:2, 0:1024], op=ALU.subtract),
                      reads=["hi32", "b32a", "b32b"], writes=[("lo32", hf)])
                P.add("dve", lambda e, src=src: e.tensor_tensor(scrB[1][0:2, 0:1024], scrB[1][0:2, 0:1024], src,
                                                                 op=ALU.subtract),
                      reads=["hi32", ("lo32", hf)], writes=["hi32"])
                P.add("dve", lambda e, sl=sl, src=src: e.scalar_tensor_tensor(
                    out=bhl[0:2, sl], in0=scrB[1][0:2, 0:1024], scalar=cols[0:2, 72:73], in1=src,
                    op0=ALU.mult, op1=ALU.add),
                    reads=["hi32", ("lo32", hf), "cols"], writes=["bhl"])
        if 0 in layers:
            init_layer0()

        def load_ln_consts(idx):
            P.add("sp", lambda e: e.dma_start(out=lnc[:, 0, :], in_=row_bc(idx * 1024, 1024)),
                  writes=["lnc_g"], is_dma=True, stream="lncg")
            P.add("sp", lambda e: e.dma_start(out=lnc[:, 1, :], in_=row_bc((4 + idx) * 1024, 1024)),
                  writes=["lnc_b"], is_dma=True, stream="lncb")

        def transpose_to_xT(xs, tt):
            bks = (next_bank(), next_bank())
            src_bf = xb[xs]

            def tr(e):
                ins = None
                for c in range(8):
                    ins = e.matmul(psum[bks[c // 4]][:, (c % 4) * 128:(c % 4 + 1) * 128],
                                   src_bf[:, c * 128:(c + 1) * 128], ident[:], start=True, stop=True)
                return ins
            P.add("pe", tr, reads=[("xb", xs), "ident"], writes=[("ps", bks[0]), ("ps", bks[1])])
            for q in range(2):
                P.add("act", lambda e, q=q: e.activation(
                    out=xT[:, 4 * q:4 * q + 4, tt * 128:(tt + 1) * 128],
                    in_=psum[bks[q]][:].rearrange("p (c t) -> p c t", c=4), func=AF.Copy),
                    reads=[("ps", bks[q])], writes=[("xT", tt, q)])

        def flush_T(keep=0):
            while len(pend_T) > keep:
                xs, tt = pend_T.pop(0)
                transpose_to_xT(xs, tt)

        def emit_ln(h, tt, final, fast=False, act_norm=False):
            z = xres[:, tt, :]
            zk = ("xres", tt)
            s = state["stat"] % 2
            state["stat"] += 1
            P.add("dve", lambda e: e.bn_stats(st[s][:, 0, :], z[:, 0:512]), reads=[zk], writes=[("st", s, 0)])
            P.add("dve", lambda e: e.bn_stats(st[s][:, 1, :], z[:, 512:1024]), reads=[zk], writes=[("st", s, 1)])
            P.add("dve", lambda e: e.bn_aggr(mv[s][:], st[s][:, 0:2, :]), reads=[("st", s, 0), ("st", s, 1)],
                  writes=[("mv", s)])
            P.add("act", lambda e: e.activation(out=rs[s][:], in_=mv[s][:, 1:2], func=AF.Sqrt, bias=EPS),
                  reads=[("mv", s)], writes=[("rs", s)])
            P.add("dve", lambda e: e.reciprocal(rs[s][:], rs[s][:]), reads=[("rs", s)], writes=[("rs", s)])
            if act_norm:
                P.add("dve", lambda e: e.scalar_tensor_tensor(out=nm[s][:], in0=mv[s][:, 0:1], scalar=-1.0, in1=rs[s][:],
                                                               op0=ALU.mult, op1=ALU.mult),
                      reads=[("mv", s), ("rs", s)], writes=[("nm", s)])
                P.add("act", lambda e: e.activation(out=z, in_=z, func=AF.Identity, scale=rs[s][:], bias=nm[s][:]),
                      reads=[zk, ("nm", s), ("rs", s)], writes=[zk])
            else:
                P.add("dve", lambda e: e.tensor_scalar(z, z, mv[s][:, 0:1], rs[s][:], op0=ALU.subtract, op1=ALU.mult),
                      reads=[zk, ("mv", s), ("rs", s)], writes=[zk])
            P.add("dve", lambda e: e.tensor_tensor(z, z, lnc[:, 0, :], op=ALU.mult), reads=[zk, "lnc_g"], writes=[zk])
            if not final:
                P.add("dve" if fast else "pool", lambda e: e.tensor_tensor(z, z, lnc[:, 1, :], op=ALU.add),
                      reads=[zk, "lnc_b"], writes=[zk])
                xs = state["xb"] % 3
                state["xb"] += 1
                P.add("act", lambda e: e.activation(out=xb[xs][:], in_=z, func=AF.Copy), reads=[zk],
                      writes=[("xb", xs)])
                pend_T.append((xs, tt))
            else:
                o = state["ost"] % 2
                state["ost"] += 1
                P.add("dve" if fast else "pool", lambda e: e.tensor_tensor(scrB[o][:], z, lnc[:, 1, :], op=ALU.add),
                      reads=[zk, "lnc_b"], writes=[("scrB", o)])
                r0 = (h * NT + tt) * 128
                P.add("sp", lambda e: e.dma_start(out=out_ap[r0:r0 + 128, :], in_=scrB[o][:]),
                      reads=[("scrB", o)], is_dma=True, stream=("out", o))

        def gemm_F(slot, tb, nk=8):
            bk = next_bank()

            def mm(e):
                ins = None
                for kc in range(nk):
                    ins = e.matmul(psum[bk][:], wsm[slot][:, kc, :], xT[:, kc, tb * 512:(tb + 1) * 512],
                                   start=(kc == 0), stop=(kc == nk - 1))
                return ins
            P.add("pe", mm, reads=[("ws", slot)] + [("xT", tb * 4 + i, q) for i in range(4) for q in range(2)],
                  writes=[("ps", bk)])
            return bk

        def gemm_T(tt, slot, ent0, nk, hchunks=True, bias_cb=None):
            bk = next_bank()
            A = H if hchunks else xT

            def mm(e):
                ins = None
                for kc in range(nk):
                    ins = e.matmul(psum[bk][:], A[:, kc, tt * 128:(tt + 1) * 128], wbig[slot][:, ent0 + kc, :],
                                   start=(kc == 0), stop=(kc == nk - 1 and bias_cb is None))
                if bias_cb is not None:
                    ins = e.matmul(psum[bk][:], ones_bf[0:2, :], bhl[0:2, bias_cb * 512:(bias_cb + 1) * 512],
                                   start=False, stop=True)
                return ins
            rd = [("wb", slot, q) for q in range(ent0 // 4, (ent0 + nk - 1) // 4 + 1)]
            rd += ([("H", kc, tt) for kc in range(nk)] if hchunks else [("xT", tt, 0), ("xT", tt, 1)])
            if bias_cb is not None:
                rd += ["bhl", "ones"]
            P.add("pe", mm, reads=rd, writes=[("ps", bk)])
            pump_big(1)
            return bk

        def f_phase(blocks, evac, first_after_ln, tb1_only=0):
            n = len(blocks)
            issue_big(state["big_use"] + 3)
            if tb1_only:
                flush_T()
            for p0 in range(0, n, 2):
                pair = list(range(p0, min(p0 + 2, n)))
                slots = {}
                for k, i in enumerate(pair):
                    slots[i] = use_sm(blocks[i], la=3 - k)
                for tb in range(2):
                    if tb == 0 and p0 < tb1_only:
                        continue
                    for i in pair:
                        bk = gemm_F(slots[i], tb)
                        evac(i, tb, bk)
                        if first_after_ln and p0 == 0 and tb == 0:
                            flush_T(keep=(1 if i == pair[0] and len(pair) > 1 else 0))
            flush_T()

        def mlp_filler(l):
            for i in range(NFILL):
                slot = use_sm(SM_BASE[f"w1_{l}"] + i)
                bk = gemm_F(slot, 0)
                ts = state["t1"] % 2
                state["t1"] += 1
                P.add("act", lambda e, bk=bk, ts=ts: e.activation(out=t1[ts][:], in_=psum[bk][:], func=AF.Relu),
                      reads=[("ps", bk)], writes=[("t1", ts, j) for j in range(4)])
                P.add("act", lambda e, i=i, ts=ts: e.activation(out=H[:, i, 0:512], in_=t1[ts][:], func=AF.Square),
                      reads=[("t1", ts, j) for j in range(4)], writes=[("H", i, k) for k in range(4)])

        def load_x_bf(h, tt, defer=False):
            r0 = (h * NT + tt) * 128
            xs = state["xb"] % 3
            state["xb"] += 1
            P.add("pool", lambda e: e.dma_start(out=xb[xs][:], in_=x_ap[r0:r0 + 128, :]),
                  writes=[("xb", xs)], is_dma=True, stream=("xbl", xs))
            if defer:
                pend_T.append((xs, tt))
            else:
                transpose_to_xT(xs, tt)

        def load_x_res(h, tt):
            r0 = (h * NT + tt) * 128
            P.add("sp", lambda e: e.dma_start(out=xres[:, tt, :], in_=x_ap[r0:r0 + 128, :]),
                  writes=[("xres", tt)], is_dma=True, stream=("xr", tt))

        for h in range(2):
            if h == 0:
                for tt in range(NT):
                    load_x_bf(h, tt)
                    load_x_res(h, tt)

            for li, l in enumerate(layers):
                last_layer = (li == len(layers) - 1)
                if l == 0:
                    load_ln_consts(0)
                    P.add("sp", lambda e: e.dma_start(out=scrB[1][:], in_=row_bc(8 * 1024, 1024)),
                          writes=[("scrB", 1)], is_dma=True, stream="c2")
                    for tt in range(NT):
                        P.add("dve", lambda e, tt=tt: e.scalar_tensor_tensor(
                            out=xres[:, tt, :], in0=xres[:, tt, :], scalar=ALPHA, in1=scrB[1][:],
                            op0=ALU.mult, op1=ALU.add),
                            reads=[("xres", tt), ("scrB", 1)], writes=[("xres", tt)])

                    def evac_u(fc, tb, bk):
                        P.add("act", lambda e: e.activation(
                            out=H[:, fc, tb * 512:(tb + 1) * 512], in_=psum[bk][:], func=AF.Gelu,
                            bias=cols[:, fc:fc + 1]),
                            reads=[("ps", bk), "cols"], writes=[("H", fc, tb * 4 + i) for i in range(4)])
                    f_phase([SM_BASE["a_u"] + fc for fc in range(16)], evac_u, first_after_ln=(li > 0))
                    vs0 = use_big(BIG_BASE["a_v"], la=3)
                    vs1 = use_big(BIG_BASE["a_v"] + 1, la=2)
                    os0 = use_big(BIG_BASE["a_out"], la=1)
                    os1 = use_big(BIG_BASE["a_out"] + 1, la=0)
                    vslots = (vs0, vs1)

                    def v_gemm(tt):
                        v = state["vslot"] % 2
                        state["vslot"] += 1
                        vraw = scrA[0]
                        vk = "scrA0"
                        for cb in range(4):
                            bk = gemm_T(tt, vslots[cb // 2], (cb % 2) * 8, 8, hchunks=False, bias_cb=cb)
                            P.add("act", lambda e, cb=cb, bk=bk: e.activation(
                                out=vraw[:, cb * 512:(cb + 1) * 512], in_=psum[bk][:], func=AF.Gelu),
                                reads=[("ps", bk)], writes=[(vk, cb)])
                        s = state["stat"] % 2
                        state["stat"] += 1
                        for q in range(4):
                            P.add("dve", lambda e, q=q: e.bn_stats(st[s][:, q, :], vraw[:, q * 512:(q + 1) * 512]),
                                  reads=[(vk, q)], writes=[("st", s, q)])
                        P.add("dve", lambda e: e.bn_aggr(mv[s][:], st[s][:]), reads=[("st", s, q) for q in range(4)],
                              writes=[("mv", s)])
                        P.add("act", lambda e: e.activation(out=rs[s][:], in_=mv[s][:, 1:2], func=AF.Sqrt, bias=EPS),
                              reads=[("mv", s)], writes=[("rs", s)])
                        P.add("dve", lambda e: e.reciprocal(rs[s][:], rs[s][:]), reads=[("rs", s)], writes=[("rs", s)])
                        P.add("dve", lambda e: e.tensor_scalar(vn[v][:], vraw[:, 0:2048], mv[s][:, 0:1], rs[s][:],
                                                                op0=ALU.subtract, op1=ALU.mult),
                              reads=[(vk, q) for q in range(4)] + [("mv", s), ("rs", s)], writes=[("scrB", v)])
                        return v

                    def sgu(tt, v):
                        for q in range(4):
                            bk = next_bank()

                            def mm(e, q=q, bk=bk):
                                ins = None
                                for j in range(4):
                                    c = 4 * q + j
                                    ins = e.matmul(psum[bk][:, j * 128:(j + 1) * 128], vn[v][:, c * 128:(c + 1) * 128],
                                                   wsT[:, c // 2, :], start=True, stop=True)
                                return ins
                            P.add("pe", mm, reads=[("scrB", v), "wsT"], writes=[("ps", bk)])
                            ts = state["t1"] % 2
                            state["t1"] += 1
                            for j in range(4):
                                c = 4 * q + j
                                P.add("dve", lambda e, j=j, c=c, bk=bk, ts=ts: e.scalar_tensor_tensor(
                                    out=t1[ts][:, j * 128:(j + 1) * 128], in0=psum[bk][:, j * 128:(j + 1) * 128],
                                    scalar=cols[:, 16 + c:17 + c], in1=bias2[:, c, :], op0=ALU.mult, op1=ALU.add),
                                    reads=[("ps", bk), "cols", ("bias2", c)], writes=[("t1", ts, j)])
                            hk = [("H", 4 * q + j, tt) for j in range(4)]
                            P.add("pool", lambda e, q=q, ts=ts: e.tensor_tensor(
                                H[:, 4 * q:4 * q + 4, tt * 128:(tt + 1) * 128],
                                t1[ts][:].rearrange("p (c t) -> p c t", c=4),
                                H[:, 4 * q:4 * q + 4, tt * 128:(tt + 1) * 128], op=ALU.mult),
                                reads=[("t1", ts, j) for j in range(4)] + hk, writes=hk)

                    def out_proj(tt):
                        for cb, slot in ((0, os0), (1, os1)):
                            bk = gemm_T(tt, slot, 0, 16)
                            P.add("dve", lambda e, cb=cb, bk=bk: e.tensor_tensor(
                                xres[:, tt, cb * 512:(cb + 1) * 512], xres[:, tt, cb * 512:(cb + 1) * 512], psum[bk][:],
                                op=ALU.add), reads=[("ps", bk), ("xres", tt)], writes=[("xres", tt)])
                        emit_ln(h, tt, False, fast=(tt >= 6))
                        flush_T(keep=2)

                    pend = None
                    for tt in range(NT):
                        v = v_gemm(tt)
                        if pend is not None:
                            sgu(*pend)
                        pend = (tt, v)
                    out_proj(0)
                    sgu(*pend)
                    for tt in range(1, NT):
                        out_proj(tt)
                else:
                    load_ln_consts(2)
                    hc = scrA[0]
                    issue_big(state["big_use"] + 3)
                    for dc in range(8):
                        banks = {}
                        sl_c = use_sm(SM_BASE["b_in"] + 8 + dc, la=3)
                        sl_h = use_sm(SM_BASE["b_in"] + 16 + dc, la=2)
                        if h == 0:
                            P.add("dve", lambda e: e.memset(hc[:, 0:2], 0.0), writes=[("hc", 0)])
                        else:
                            P.add("dve", lambda e, dc=dc: e.tensor_copy(hc[:, 0:2], convst[:, dc, :]),
                                  reads=[("convst", dc)], writes=[("hc", 0)])
                        for tb in range(2):
                            banks[("c", tb)] = gemm_F(sl_c, tb)
                            if dc == 0 and tb == 0 and li > 0:
                                flush_T(keep=1)
                            banks[("h", tb)] = gemm_F(sl_h, tb)
                            if dc == 0 and tb == 0:
                                flush_T()
                            bc_, bh_ = banks[("c", tb)], banks[("h", tb)]
                            sl = slice(tb * 512, (tb + 1) * 512)
                            P.add("act", lambda e, sl=sl, bc_=bc_: e.activation(out=scrB[0][:, sl], in_=psum[bc_][:],
                                                                                func=AF.Copy),
                                  reads=[("ps", bc_)], writes=[("csb", tb)])
                            P.add("dve", lambda e, tb=tb, sl=sl, bh_=bh_: e.tensor_tensor(
                                hc[:, 2 + tb * 512:2 + (tb + 1) * 512], psum[bh_][:], scrB[0][:, sl], op=ALU.mult),
                                reads=[("ps", bh_), ("csb", tb)], writes=[("hc", 1 + tb)])
                        sl_b = use_sm(SM_BASE["b_in"] + dc, la=3)
                        for tb in range(2):
                            banks[("b", tb)] = gemm_F(sl_b, tb)
                        if h == 0:
                            P.add("dve", lambda e, dc=dc: e.tensor_copy(convst[:, dc, :], hc[:, 1024:1026]),
                                  reads=[("hc", 2)], writes=[("convst", dc)])
                        for tb in range(2):
                            bb_ = banks[("b", tb)]
                            sl = slice(tb * 512, (tb + 1) * 512)
                            asl = slice(tb * 512, (tb + 1) * 512)
                            rdk = [("hc", 0), ("hc", 1), ("hc", 2)] if tb == 1 else [("hc", 0), ("hc", 1)]
                            P.add("act", lambda e, tb=tb, asl=asl, dc=dc: e.activation(
                                out=scrB[1][:, asl], in_=hc[:, 2 + tb * 512:2 + (tb + 1) * 512], func=AF.Identity,
                                scale=cols[:, 48 + 16 + dc:48 + 17 + dc]),
                                reads=rdk + ["cols"], writes=[("acc", tb)])
                            for k in (1, 0):
                                P.add("dve", lambda e, tb=tb, asl=asl, dc=dc, k=k: e.scalar_tensor_tensor(
                                    out=scrB[1][:, asl], in0=hc[:, k + tb * 512:k + (tb + 1) * 512],
                                    scalar=cols[:, 48 + k * 8 + dc:48 + k * 8 + dc + 1], in1=scrB[1][:, asl],
                                    op0=ALU.mult, op1=ALU.add),
                                    reads=rdk + ["cols", ("acc", tb)], writes=[("acc", tb)])
                            P.add("dve", lambda e, tb=tb, sl=sl, asl=asl, dc=dc, bb_=bb_: e.tensor_tensor(
                                H[:, dc, sl], psum[bb_][:], scrB[1][:, asl], op=ALU.mult),
                                reads=[("ps", bb_), ("acc", tb)], writes=[("H", dc, tb * 4 + i) for i in range(4)])
                    slot = use_big(BIG_BASE["b_out"])
                    for tt in range(NT):
                        for cb in range(2):
                            bk = gemm_T(tt, slot, cb * 8, 8)
                            P.add("dve", lambda e, tt=tt, cb=cb, bk=bk: e.scalar_tensor_tensor(
                                out=xres[:, tt, cb * 512:(cb + 1) * 512], in0=xres[:, tt, cb * 512:(cb + 1) * 512],
                                scalar=ALPHA, in1=psum[bk][:], op0=ALU.mult, op1=ALU.add),
                                reads=[("ps", bk), ("xres", tt)], writes=[("xres", tt)])
                        emit_ln(h, tt, False, fast=(tt >= 6), act_norm=True)
                        flush_T(keep=2)

                mlp_filler(l)
                load_ln_consts(l * 2 + 1)
                for fh in range(2):
                    def evac_h(fc, tb, bk):
                        ts = state["t1"] % 2
                        state["t1"] += 1
                        P.add("act", lambda e: e.activation(out=t1[ts][:], in_=psum[bk][:], func=AF.Relu),
                              reads=[("ps", bk)], writes=[("t1", ts, j) for j in range(4)])
                        P.add("dve", lambda e: e.tensor_tensor(
                            H[:, fc, tb * 512:(tb + 1) * 512], t1[ts][:], t1[ts][:], op=ALU.mult),
                            reads=[("t1", ts, j) for j in range(4)],
                            writes=[("H", fc, tb * 4 + i) for i in range(4)])
                    f_phase([SM_BASE[f"w1_{l}"] + fh * 16 + fc for fc in range(16)], evac_h,
                            first_after_ln=(fh == 0), tb1_only=(NFILL if fh == 0 else 0))
                    s0 = use_big(BIG_BASE[f"w2_{l}"] + fh * 2 + 0)
                    s1 = use_big(BIG_BASE[f"w2_{l}"] + fh * 2 + 1, la=2)
                    for tt in range(NT):
                        for cb, slot in ((0, s0), (1, s1)):
                            bk = gemm_T(tt, slot, 0, 16)
                            if fh == 0:
                                P.add("dve", lambda e, tt=tt, cb=cb, bk=bk: e.scalar_tensor_tensor(
                                    out=xres[:, tt, cb * 512:(cb + 1) * 512], in0=xres[:, tt, cb * 512:(cb + 1) * 512],
                                    scalar=ALPHA, in1=psum[bk][:], op0=ALU.mult, op1=ALU.add),
                                    reads=[("ps", bk), ("xres", tt)], writes=[("xres", tt)])
                            else:
                                P.add("dve", lambda e, tt=tt, cb=cb, bk=bk: e.tensor_tensor(
                                    xres[:, tt, cb * 512:(cb + 1) * 512], xres[:, tt, cb * 512:(cb + 1) * 512],
                                    psum[bk][:], op=ALU.add), reads=[("ps", bk), ("xres", tt)], writes=[("xres", tt)])
                        if fh == 1:
                            emit_ln(h, tt, last_layer, fast=(tt >= 6))
                            flush_T(keep=2)
                            if last_layer and h == 0:
                                load_x_bf(1, tt, defer=True)
                                load_x_res(1, tt)
            flush_T()

        assert state["sm_use"] == len(sm_plan) and state["big_use"] == len(big_plan) and not big_pending
        P.finalize()

        streams = list(P.dma_streams.keys())
        sem_eng = {e: es.enter_context(nc.semaphore(f"s_{e}")) for e in ENGINES}
        sem_dma = {s_: es.enter_context(nc.semaphore("d_" + "_".join(str(t) for t in (s_ if isinstance(s_, tuple) else (s_,)))))
                   for s_ in streams}
        block = es.enter_context(nc.Block())

        @block.tensor
        def _(e):
            P.emit_engine("pe", e, sem_eng, sem_dma)

        @block.scalar
        def _(e):
            P.emit_engine("act", e, sem_eng, sem_dma)

        @block.vector
        def _(e):
            P.emit_engine("dve", e, sem_eng, sem_dma)

        @block.gpsimd
        def _(e):
            P.emit_engine("pool", e, sem_eng, sem_dma)

        @block.sync
        def _(e):
            P.emit_engine("sp", e, sem_eng, sem_dma)
            for s_ in streams:
                if isinstance(s_, tuple) and s_[0] == "out":
                    e.wait_ge(sem_dma[s_], 16 * P.dma_streams[s_])
    return nc


_NC_CACHE = {}


def _get_nc(layers):
    if layers not in _NC_CACHE:
        _NC_CACHE[layers] = build_nc(layers)
    return _NC_CACHE[layers]


def _run(layers, x, wts):
    nc = _get_nc(layers)
    in_maps = []
    for b in range(NB):
        m = {"x": np.ascontiguousarray(x[b])}
        m.update(wts)
        in_maps.append(m)
    res = run_bass_kernel_spmd(nc, in_maps, core_ids=list(range(NB)))
    return np.stack([np.asarray(r["out"]) for r in res.results], axis=0)


FUSED = True


def kernel(**inputs):
    x = np.asarray(inputs["x"], dtype=np.float32)
    wts = prepare_weights(inputs)
    if FUSED:
        out = _run((0, 1), x, wts)
    else:
        mid = _run((0,), x, wts)
        out = _run((1,), mid, wts)
    return out.astype(np.float32)
```

```python
from contextlib import ExitStack
import numpy as np
import concourse.bass as bass
import concourse.mybir as mybir
from concourse.bass_utils import run_bass_kernel_spmd

F32 = mybir.dt.float32
BF16 = mybir.dt.bfloat16
AF = mybir.ActivationFunctionType
ALU = mybir.AluOpType

D = 1024
SEQ = 2048
NB = 8
TH = 1024
NT = TH // 128
ALPHA = float((2.0 * 2) ** 0.25)
EPS = 1e-5
ENGINES = ("pe", "act", "dve", "pool", "sp")


ALIAS = {
    ("scrB", 0): (("B", 0, 0), ("B", 0, 1)), ("scrB", 1): (("B", 1, 0), ("B", 1, 1)),
    ("hib", 0): (("B", 0, 0),), ("hib", 1): (("B", 0, 1),),
    "hi32": (("B", 1, 0), ("B", 1, 1)),
    ("csb", 0): (("B", 0, 0),), ("csb", 1): (("B", 0, 1),),
    ("acc", 0): (("B", 1, 0),), ("acc", 1): (("B", 1, 1),),
    "b32a": (("A", 0), ("A", 1)), "b32b": (("A", 2), ("A", 3)),
    ("lo32", 0): (("A", 0), ("A", 1)), ("lo32", 1): (("A", 2), ("A", 3)),
    ("scrA0", 0): (("A", 0),), ("scrA0", 1): (("A", 1),), ("scrA0", 2): (("A", 2),), ("scrA0", 3): (("A", 3),),
    ("hc", 0): (("A", 0),), ("hc", 1): (("A", 0), ("A", 1)), ("hc", 2): (("A", 1), ("A", 2)),
}
class Op:
    __slots__ = ("eng", "fn", "reads", "writes", "is_dma", "stream", "idx", "eidx",
                 "waits", "signal", "sig_val", "dma_val")

    def __init__(self, eng, fn, reads, writes, is_dma, stream):
        self.eng = eng
        self.fn = fn
        self.reads = reads
        self.writes = writes
        self.is_dma = is_dma
        self.stream = stream
        self.waits = []
        self.signal = False
        self.sig_val = None
        self.dma_val = None


class Prog:
    def __init__(self):
        self.ops = []
        self.eng_ops = {e: [] for e in ENGINES}
        self.last_writer = {}
        self.readers = {}
        self.dma_streams = {}
        self.waited = {e: {} for e in ENGINES}
        self.bank_busy = {}

    def _expand(self, keys):
        out = []
        for k in keys:
            out.extend(ALIAS.get(k, (k,)))
        return tuple(out)

    def add(self, eng, fn, reads=(), writes=(), is_dma=False, stream=None):
        op = Op(eng, fn, self._expand(reads), self._expand(writes), is_dma, stream)
        for k in op.writes:
            if isinstance(k, tuple) and k[0] == "ps" and eng == "pe":
                assert not self.bank_busy.get(k[1], False), ("PSUM bank reused before its evacuation was emitted", k)
                self.bank_busy[k[1]] = True
        for k in op.reads:
            if isinstance(k, tuple) and k[0] == "ps" and eng != "pe":
                self.bank_busy[k[1]] = False
        op.idx = len(self.ops)
        op.eidx = len(self.eng_ops[eng])
        self.ops.append(op)
        self.eng_ops[eng].append(op)
        if is_dma:
            c = self.dma_streams.get(stream, 0) + 1
            self.dma_streams[stream] = c
            op.dma_val = 16 * c
        deps = {}

        def need(src, raw):
            if src is None or src is op:
                return
            prev = deps.get(id(src))
            deps[id(src)] = (src, raw if prev is None else (prev[1] or raw))

        for k in op.reads:
            need(self.last_writer.get(k), True)
        for k in op.writes:
            need(self.last_writer.get(k), False)
            rd = self.readers.get(k)
            if rd:
                for r in rd.values():
                    need(r, False)
        best = {}
        for src, raw in deps.values():
            if src.is_dma:
                key = ("dma", src.stream)
                if src.dma_val > self.waited[eng].get(key, 0):
                    self.waited[eng][key] = src.dma_val
                    op.waits.append(("dma", src.stream, src.dma_val))
                continue
            if src.eng == eng and not op.is_dma:
                if eng == "pe" or not raw:
                    continue
            cur = best.get(src.eng)
            if cur is None or src.eidx > cur.eidx:
                best[src.eng] = src
        for se, src in best.items():
            key = ("eng", se)
            if src.eidx > self.waited[eng].get(key, -1):
                self.waited[eng][key] = src.eidx
                src.signal = True
                op.waits.append(("eng", src, None))
        for k in op.writes:
            self.last_writer[k] = op
            self.readers[k] = {}
        for k in op.reads:
            d = self.readers.setdefault(k, {})
            d[eng if not is_dma else ("dma", op.idx)] = op
        return op

    def finalize(self):
        for e in ENGINES:
            c = 0
            for op in self.eng_ops[e]:
                if op.signal:
                    assert not op.is_dma
                    c += 1
                    op.sig_val = c

    def emit_engine(self, name, eng, sem_eng, sem_dma):
        for op in self.eng_ops[name]:
            for kind, a, b in op.waits:
                if kind == "dma":
                    eng.wait_ge(sem_dma[a], b)
                else:
                    eng.wait_ge(sem_eng[a.eng], a.sig_val)
            ins = op.fn(eng)
            if op.is_dma:
                ins.then_inc(sem_dma[op.stream], 16)
            elif op.signal:
                ins.then_inc(sem_eng[name], 1)


def _fblocks(w):
    K, N = w.shape
    nf = N // 128
    return np.ascontiguousarray(w.reshape(K // 128, 128, nf, 128).transpose(2, 1, 0, 3).reshape(nf, 128, (K // 128) * 128))


def _tblock(w, r0, nr, c0):
    nk = nr // 128
    return w[r0:r0 + nr, c0:c0 + 512].reshape(nk, 128, 512).transpose(1, 0, 2).reshape(128, nk * 512)


def prepare_weights(inp):
    f = lambda a: np.asarray(a, dtype=np.float32)
    a_w_in = f(inp["a_w_in"])[0]
    a_w_out = f(inp["a_w_out"])[0]
    b_w_in = f(inp["b_w_in"])[0]
    b_w_out = f(inp["b_w_out"])[0]
    w1 = f(inp["mlp_w1"])
    w2 = f(inp["mlp_w2"])
    wsm = np.concatenate([_fblocks(a_w_in[:, :2048]), _fblocks(b_w_in), _fblocks(w1[0]), _fblocks(w1[1])], axis=0)
    big = []
    for i in range(2):
        big.append(np.concatenate([_tblock(a_w_in, 0, 1024, 2048 + (2 * i) * 512),
                                   _tblock(a_w_in, 0, 1024, 2048 + (2 * i + 1) * 512)], axis=1))
    for cb in range(2):
        big.append(_tblock(a_w_out, 0, 2048, cb * 512))
    for l in range(2):
        for fh in range(2):
            for cb in range(2):
                big.append(_tblock(w2[l], fh * 2048, 2048, cb * 512))
    big.append(np.concatenate([_tblock(b_w_out, 0, 1024, 0), _tblock(b_w_out, 0, 1024, 512)], axis=1))
    wbig = np.ascontiguousarray(np.stack(big, axis=0))
    cols = np.zeros((128, 73), np.float32)
    cols[0, 72] = 1.0
    b_in = f(inp["a_b_in"])[0]
    cols[:, 0:16] = b_in[:2048].reshape(16, 128).T
    cols[:, 16:32] = f(inp["a_v_g"])[0].reshape(16, 128).T
    cols[:, 32:48] = f(inp["a_v_b"])[0].reshape(16, 128).T
    conv = f(inp["b_conv"])[0]
    for k in range(3):
        cols[:, 48 + k * 8:48 + (k + 1) * 8] = conv[k].reshape(8, 128).T
    rows = np.zeros((12, 1024), np.float32)
    ln_g = f(inp["ln_g"]); ln_b = f(inp["ln_b"])
    for l in range(2):
        for s in range(2):
            rows[l * 2 + s] = ln_g[l, s]
            rows[4 + l * 2 + s] = ln_b[l, s]
    rows[8] = f(inp["a_b_out"])[0]
    rows[9:11] = b_in[2048:].reshape(2, 1024)
    rows[11] = f(inp["a_b_s"])[0].reshape(1024)
    wsT = np.ascontiguousarray(f(inp["a_w_s"])[0].transpose(2, 0, 1).reshape(128, 1024))
    return {"wsm": wsm, "wbig": wbig, "cols": cols, "rows": rows, "wsT": wsT}


SM_BASE = {"a_u": 0, "b_in": 16, "w1_0": 40, "w1_1": 72}
BIG_BASE = {"a_v": 0, "a_out": 2, "w2_0": 4, "w2_1": 8, "b_out": 12}


NCOLS = 73


def build_nc(layers=(0, 1)):
    nc = bass.Bass("TRN2", target_bir_lowering=False)
    x_d = nc.dram_tensor("x", [SEQ, D], F32, kind="ExternalInput")
    wsm_d = nc.dram_tensor("wsm", [104, 128, 1024], F32, kind="ExternalInput")
    wbig_d = nc.dram_tensor("wbig", [13, 128, 8192], F32, kind="ExternalInput")
    cols_d = nc.dram_tensor("cols", [128, NCOLS], F32, kind="ExternalInput")
    rows_d = nc.dram_tensor("rows", [12, 1024], F32, kind="ExternalInput")
    wsT_d = nc.dram_tensor("wsT", [128, 1024], F32, kind="ExternalInput")
    out_d = nc.dram_tensor("out", [SEQ, D], F32, kind="ExternalOutput")
    x_ap = x_d.ap()
    out_ap = out_d.ap()
    wsm_ap = wsm_d.ap()
    wbig_ap = wbig_d.ap()

    def row_bc(r0, n, parts=128):
        return bass.AP(rows_d, r0, [[0, parts], [1, n]])

    es = ExitStack()
    with es:
        def sb(name, shape, dt):
            return es.enter_context(nc.sbuf_tensor(name, shape, dt))

        xres = sb("xres", [128, NT, D], F32)
        xT = sb("xT", [128, 8, TH], BF16)
        H = sb("H", [128, 16, TH], BF16)
        wbig = [sb(f"wbig{i}", [128, 16, 512], BF16) for i in range(4)]
        wsm = [sb(f"wsm{i}", [128, 8, 128], BF16) for i in range(4)]
        scrA = [sb(f"scrA{i}", [128, 2050], F32) for i in range(1)]
        scrB = [sb(f"scrB{i}", [128, 1024], F32) for i in range(2)]
        t1 = [sb(f"t1_{i}", [128, 512], F32) for i in range(2)]
        xb = [sb(f"xb{i}", [128, D], BF16) for i in range(3)]
        lnc = sb("lnc", [128, 2, D], F32)
        bias2 = sb("bias2", [128, 16, 128], F32)
        bhl = sb("bhl", [2, 2048], BF16)
        wsT = sb("wsT_sb", [128, 8, 128], BF16)
        ones_bf = sb("ones_bf", [128, 128], BF16)
        ident = sb("ident", [128, 128], BF16)
        cols = sb("cols_sb", [128, NCOLS], F32)
        convst = sb("convst", [128, 8, 2], F32)
        st = [sb(f"st{i}", [128, 4, 6], F32) for i in range(2)]
        mv = [sb(f"mv{i}", [128, 2], F32) for i in range(2)]
        rs = [sb(f"rs{i}", [128, 1], F32) for i in range(2)]
        nm = [sb(f"nm{i}", [128, 1], F32) for i in range(2)]
        psum = [es.enter_context(nc.psum_tensor(f"ps{i}", [128, 512], F32)) for i in range(8)]
        psum_bf = [p.bitcast(BF16) for p in psum]
        vn = [s_.bitcast(BF16) for s_ in scrB]

        P = Prog()
        state = {"bank": 0, "sm_issue": 0, "sm_use": 0, "big_issue": 0, "big_use": 0, "t1": 0,
                 "xb": 0, "stat": 0, "vslot": 0, "ost": 0}
        big_pending = []
        pend_T = []

        def next_bank():
            b = state["bank"] % 8
            state["bank"] += 1
            return b

        sm_plan, big_plan = [], []
        for h in range(2):
            for l in layers:
                if l == 0:
                    sm_plan += [SM_BASE["a_u"] + i for i in range(16)]
                    big_plan += [BIG_BASE["a_v"], BIG_BASE["a_v"] + 1, BIG_BASE["a_out"], BIG_BASE["a_out"] + 1]
                else:
                    for dc in range(8):
                        sm_plan += [SM_BASE["b_in"] + 8 + dc, SM_BASE["b_in"] + 16 + dc, SM_BASE["b_in"] + dc]
                    big_plan += [BIG_BASE["b_out"]]
                for fh in range(2):
                    sm_plan += [SM_BASE[f"w1_{l}"] + fh * 16 + i for i in range(16)]
                    big_plan += [BIG_BASE[f"w2_{l}"] + fh * 2 + cb for cb in range(2)]

        def pump_big(n=1, upto_block=None):
            while big_pending and (n > 0 or (upto_block is not None and big_pending[0][0] <= upto_block)):
                i, slot, q, blk = big_pending.pop(0)
                P.add("pool", lambda e, slot=slot, q=q, blk=blk: e.dma_start(
                    out=wbig[slot][:, 4 * q:4 * q + 4, :].rearrange("p a b -> p (a b)"),
                    in_=wbig_ap[blk][:, q * 2048:(q + 1) * 2048]),
                    writes=[("wb", slot, q)], is_dma=True, stream=("wb", slot, q))
                n -= 1

        def issue_sm(upto):
            while state["sm_issue"] <= upto and state["sm_issue"] < len(sm_plan):
                i = state["sm_issue"]
                slot = i % 4
                blk = sm_plan[i]
                P.add("pool", lambda e, slot=slot, blk=blk: e.dma_start(
                    out=wsm[slot][:].rearrange("p a b -> p (a b)"), in_=wsm_ap[blk]),
                    writes=[("ws", slot)], is_dma=True, stream=("ws", slot))
                state["sm_issue"] += 1
                pump_big(1)

        def issue_big(upto):
            while state["big_issue"] <= upto and state["big_issue"] < len(big_plan):
                i = state["big_issue"]
                for q in range(4):
                    big_pending.append((i, i % 4, q, big_plan[i]))
                state["big_issue"] += 1

        def use_sm(blk, la=3):
            i = state["sm_use"]
            assert sm_plan[i] == blk, (i, sm_plan[i], blk)
            issue_sm(i + la)
            state["sm_use"] += 1
            return i % 4

        def use_big(blk, la=3):
            i = state["big_use"]
            assert big_plan[i] == blk, (i, big_plan[i], blk)
            issue_big(i + la)
            pump_big(0, upto_block=i)
            state["big_use"] += 1
            return i % 4

        P.add("sp", lambda e: e.dma_start(out=cols[:], in_=cols_d.ap()), writes=["cols"], is_dma=True, stream="c0")
        identf = scrB[0]
        P.add("dve", lambda e: e.memset(identf[:, 0:128], 1.0), writes=[("scrB", 0)])
        P.add("pool", lambda e: e.affine_select(out=identf[:, 0:128], in_=identf[:, 0:128], pattern=[[-1, 128]],
                                                  compare_op=ALU.is_equal, fill=0.0, base=0, channel_multiplier=1),
              reads=[("scrB", 0)], writes=[("scrB", 0)])
        P.add("dve", lambda e: e.tensor_copy(ident[:], identf[:, 0:128]), reads=[("scrB", 0)], writes=["ident"])
        def init_layer0():
            P.add("dve", lambda e: e.memset(ones_bf[:], 1.0), writes=["ones"])
            P.add("pool", lambda e: e.dma_start(out=wsT[:].rearrange("p a b -> p (a b)"), in_=wsT_d.ap()),
                  writes=["wsT"], is_dma=True, stream="c1")
            P.add("dve", lambda e: e.memset(wsT[64:128, :, 0:64], 0.0), reads=["wsT"], writes=["wsT"])
            bsbc = lnc[:, 1, :]
            P.add("sp", lambda e: e.dma_start(out=lnc[:, 1, :], in_=row_bc(11 * 1024, 1024)), writes=["lnc_b"],
                  is_dma=True, stream="lncb")
            for g in range(8):
                bk = next_bank()
                P.add("pe", lambda e, g=g, bk=bk: e.matmul(psum[bk][:, 0:128], ones_bf[:], wsT[:, g, :],
                                                           start=True, stop=True),
                      reads=["ones", "wsT"], writes=[("ps", bk)])
                for j in range(2):
                    c = 2 * g + j
                    P.add("dve", lambda e, g=g, c=c, bk=bk: e.scalar_tensor_tensor(
                        out=bias2[:, c, :], in0=psum[bk][:, 0:128], scalar=cols[:, 32 + c:33 + c],
                        in1=lnc[:, 1, g * 128:(g + 1) * 128], op0=ALU.mult, op1=ALU.add),
                        reads=[("ps", bk), "cols", "lnc_b"], writes=[("bias2", c)])

            b32 = scrA[0]
            P.add("sp", lambda e: e.dma_start(out=b32[0:2, 0:1024], in_=row_bc(9 * 1024, 1024, parts=2)),
                  writes=["b32a"], is_dma=True, stream="c3")
            P.add("sp", lambda e: e.dma_start(out=b32[0:2, 1024:2048], in_=row_bc(10 * 1024, 1024, parts=2)),
                  writes=["b32b"], is_dma=True, stream="c5")
            for hf in range(2):
                sl = slice(hf * 1024, (hf + 1) * 1024)
                src = b32[0:2, hf * 1024:(hf + 1) * 1024]
                P.add("dve", lambda e, sl=sl, src=src: e.tensor_copy(vn[0][0:2, sl], src),
                      reads=["b32a", "b32b", "ident"], writes=[("hib", hf), ("scrB", 0)])
                P.add("dve", lambda e, sl=sl: e.tensor_copy(scrB[1][0:2, 0:1024], vn[0][0:2, sl]),
                      reads=[("hib", hf)], writes=["hi32", ("scrB", 1)])
                P.add("dve", lambda e, src=src: e.tensor_tensor(src, src, scrB[1][0:2, 0:1024], op=ALU.subtract),
                      reads=["hi32", "b32a", "b32b"], writes=[("lo32", hf)])
                P.add("dve", lambda e, src=src: e.tensor_tensor(scrB[1][0:2, 0:1024], scrB[1][0:2, 0:1024], src,
                                                                 op=ALU.subtract),
                      reads=["hi32", ("lo32", hf)], writes=["hi32"])
                P.add("dve", lambda e, sl=sl, src=src: e.scalar_tensor_tensor(
                    out=bhl[0:2, sl], in0=scrB[1][0:2, 0:1024], scalar=cols[0:2, 72:73], in1=src,
                    op0=ALU.mult, op1=ALU.add),
                    reads=["hi32", ("lo32", hf), "cols"], writes=["bhl"])
        if 0 in layers:
            init_layer0()

        def load_ln_consts(idx):
            P.add("sp", lambda e: e.dma_start(out=lnc[:, 0, :], in_=row_bc(idx * 1024, 1024)),
                  writes=["lnc_g"], is_dma=True, stream="lncg")
            P.add("sp", lambda e: e.dma_start(out=lnc[:, 1, :], in_=row_bc((4 + idx) * 1024, 1024)),
                  writes=["lnc_b"], is_dma=True, stream="lncb")

        def transpose_to_xT(xs, tt):
            bks = (next_bank(), next_bank())
            src_bf = xb[xs]

            def tr(e):
                ins = None
                for c in range(8):
                    ins = e.matmul(psum[bks[c // 4]][:, (c % 4) * 128:(c % 4 + 1) * 128],
                                   src_bf[:, c * 128:(c + 1) * 128], ident[:], start=True, stop=True)
                return ins
            P.add("pe", tr, reads=[("xb", xs), "ident"], writes=[("ps", bks[0]), ("ps", bks[1])])
            for q in range(2):
                P.add("act", lambda e, q=q: e.activation(
                    out=xT[:, 4 * q:4 * q + 4, tt * 128:(tt + 1) * 128],
                    in_=psum[bks[q]][:].rearrange("p (c t) -> p c t", c=4), func=AF.Copy),
                    reads=[("ps", bks[q])], writes=[("xT", tt, q)])

        def flush_T(keep=0):
            while len(pend_T) > keep:
                xs, tt = pend_T.pop(0)
                transpose_to_xT(xs, tt)

        def emit_ln(h, tt, final, fast=False, act_norm=False):
            z = xres[:, tt, :]
            zk = ("xres", tt)
            s = state["stat"] % 2
            state["stat"] += 1
            P.add("dve", lambda e: e.bn_stats(st[s][:, 0, :], z[:, 0:512]), reads=[zk], writes=[("st", s, 0)])
            P.add("dve", lambda e: e.bn_stats(st[s][:, 1, :], z[:, 512:1024]), reads=[zk], writes=[("st", s, 1)])
            P.add("dve", lambda e: e.bn_aggr(mv[s][:], st[s][:, 0:2, :]), reads=[("st", s, 0), ("st", s, 1)],
                  writes=[("mv", s)])
            P.add("act", lambda e: e.activation(out=rs[s][:], in_=mv[s][:, 1:2], func=AF.Sqrt, bias=EPS),
                  reads=[("mv", s)], writes=[("rs", s)])
            P.add("dve", lambda e: e.reciprocal(rs[s][:], rs[s][:]), reads=[("rs", s)], writes=[("rs", s)])
            if act_norm:
                P.add("dve", lambda e: e.scalar_tensor_tensor(out=nm[s][:], in0=mv[s][:, 0:1], scalar=-1.0, in1=rs[s][:],
                                                               op0=ALU.mult, op1=ALU.mult),
                      reads=[("mv", s), ("rs", s)], writes=[("nm", s)])
                P.add("act", lambda e: e.activation(out=z, in_=z, func=AF.Identity, scale=rs[s][:], bias=nm[s][:]),
                      reads=[zk, ("nm", s), ("rs", s)], writes=[zk])
            else:
                P.add("dve", lambda e: e.tensor_scalar(z, z, mv[s][:, 0:1], rs[s][:], op0=ALU.subtract, op1=ALU.mult),
                      reads=[zk, ("mv", s), ("rs", s)], writes=[zk])
            P.add("dve", lambda e: e.tensor_tensor(z, z, lnc[:, 0, :], op=ALU.mult), reads=[zk, "lnc_g"], writes=[zk])
            if not final:
                P.add("dve" if fast else "pool", lambda e: e.tensor_tensor(z, z, lnc[:, 1, :], op=ALU.add),
                      reads=[zk, "lnc_b"], writes=[zk])
                xs = state["xb"] % 3
                state["xb"] += 1
                P.add("act", lambda e: e.activation(out=xb[xs][:], in_=z, func=AF.Copy), reads=[zk],
                      writes=[("xb", xs)])
                pend_T.append((xs, tt))
            else:
                o = state["ost"] % 2
                state["ost"] += 1
                P.add("dve" if fast else "pool", lambda e: e.tensor_tensor(scrB[o][:], z, lnc[:, 1, :], op=ALU.add),
                      reads=[zk, "lnc_b"], writes=[("scrB", o)])
                r0 = (h * NT + tt) * 128
                P.add("sp", lambda e: e.dma_start(out=out_ap[r0:r0 + 128, :], in_=scrB[o][:]),
                      reads=[("scrB", o)], is_dma=True, stream=("out", o))

        def gemm_F(slot, tb, nk=8):
            bk = next_bank()

            def mm(e):
                ins = None
                for kc in range(nk):
                    ins = e.matmul(psum[bk][:], wsm[slot][:, kc, :], xT[:, kc, tb * 512:(tb + 1) * 512],
                                   start=(kc == 0), stop=(kc == nk - 1))
                return ins
            P.add("pe", mm, reads=[("ws", slot)] + [("xT", tb * 4 + i, q) for i in range(4) for q in range(2)],
                  writes=[("ps", bk)])
            return bk

        def gemm_T(tt, slot, ent0, nk, hchunks=True, bias_cb=None):
            bk = next_bank()
            A = H if hchunks else xT

            def mm(e):
                ins = None
                for kc in range(nk):
                    ins = e.matmul(psum[bk][:], A[:, kc, tt * 128:(tt + 1) * 128], wbig[slot][:, ent0 + kc, :],
                                   start=(kc == 0), stop=(kc == nk - 1 and bias_cb is None))
                if bias_cb is not None:
                    ins = e.matmul(psum[bk][:], ones_bf[0:2, :], bhl[0:2, bias_cb * 512:(bias_cb + 1) * 512],
                                   start=False, stop=True)
                return ins
            rd = [("wb", slot, q) for q in range(ent0 // 4, (ent0 + nk - 1) // 4 + 1)]
            rd += ([("H", kc, tt) for kc in range(nk)] if hchunks else [("xT", tt, 0), ("xT", tt, 1)])
            if bias_cb is not None:
                rd += ["bhl", "ones"]
            P.add("pe", mm, reads=rd, writes=[("ps", bk)])
            pump_big(1)
            return bk

        def f_phase(blocks, evac, first_after_ln):
            n = len(blocks)
            issue_big(state["big_use"] + 3)
            for p0 in range(0, n, 2):
                pair = list(range(p0, min(p0 + 2, n)))
                slots = {}
                for k, i in enumerate(pair):
                    slots[i] = use_sm(blocks[i], la=3 - k)
                for tb in range(2):
                    for i in pair:
                        bk = gemm_F(slots[i], tb)
                        evac(i, tb, bk)
                        if first_after_ln and p0 == 0 and tb == 0:
                            flush_T(keep=(1 if i == pair[0] and len(pair) > 1 else 0))
            flush_T()

        def load_x_bf(h, tt, defer=False):
            r0 = (h * NT + tt) * 128
            xs = state["xb"] % 3
            state["xb"] += 1
            P.add("pool", lambda e: e.dma_start(out=xb[xs][:], in_=x_ap[r0:r0 + 128, :]),
                  writes=[("xb", xs)], is_dma=True, stream=("xbl", xs))
            if defer:
                pend_T.append((xs, tt))
            else:
                transpose_to_xT(xs, tt)

        def load_x_res(h, tt):
            r0 = (h * NT + tt) * 128
            P.add("sp", lambda e: e.dma_start(out=xres[:, tt, :], in_=x_ap[r0:r0 + 128, :]),
                  writes=[("xres", tt)], is_dma=True, stream=("xr", tt))

        for h in range(2):
            if h == 0:
                for tt in range(NT):
                    load_x_bf(h, tt)
                    load_x_res(h, tt)

            for li, l in enumerate(layers):
                last_layer = (li == len(layers) - 1)
                if l == 0:
                    load_ln_consts(0)
                    P.add("sp", lambda e: e.dma_start(out=scrB[1][:], in_=row_bc(8 * 1024, 1024)),
                          writes=[("scrB", 1)], is_dma=True, stream="c2")
                    for tt in range(NT):
                        P.add("dve", lambda e, tt=tt: e.scalar_tensor_tensor(
                            out=xres[:, tt, :], in0=xres[:, tt, :], scalar=ALPHA, in1=scrB[1][:],
                            op0=ALU.mult, op1=ALU.add),
                            reads=[("xres", tt), ("scrB", 1)], writes=[("xres", tt)])

                    def evac_u(fc, tb, bk):
                        P.add("act", lambda e: e.activation(
                            out=H[:, fc, tb * 512:(tb + 1) * 512], in_=psum[bk][:], func=AF.Gelu,
                            bias=cols[:, fc:fc + 1]),
                            reads=[("ps", bk), "cols"], writes=[("H", fc, tb * 4 + i) for i in range(4)])
                    f_phase([SM_BASE["a_u"] + fc for fc in range(16)], evac_u, first_after_ln=(li > 0))
                    vs0 = use_big(BIG_BASE["a_v"], la=3)
                    vs1 = use_big(BIG_BASE["a_v"] + 1, la=2)
                    os0 = use_big(BIG_BASE["a_out"], la=1)
                    os1 = use_big(BIG_BASE["a_out"] + 1, la=0)
                    vslots = (vs0, vs1)

                    def v_gemm(tt):
                        v = state["vslot"] % 2
                        state["vslot"] += 1
                        vraw = scrA[0]
                        vk = "scrA0"
                        for cb in range(4):
                            bk = gemm_T(tt, vslots[cb // 2], (cb % 2) * 8, 8, hchunks=False, bias_cb=cb)
                            P.add("act", lambda e, cb=cb, bk=bk: e.activation(
                                out=vraw[:, cb * 512:(cb + 1) * 512], in_=psum[bk][:], func=AF.Gelu),
                                reads=[("ps", bk)], writes=[(vk, cb)])
                        s = state["stat"] % 2
                        state["stat"] += 1
                        for q in range(4):
                            P.add("dve", lambda e, q=q: e.bn_stats(st[s][:, q, :], vraw[:, q * 512:(q + 1) * 512]),
                                  reads=[(vk, q)], writes=[("st", s, q)])
                        P.add("dve", lambda e: e.bn_aggr(mv[s][:], st[s][:]), reads=[("st", s, q) for q in range(4)],
                              writes=[("mv", s)])
                        P.add("act", lambda e: e.activation(out=rs[s][:], in_=mv[s][:, 1:2], func=AF.Sqrt, bias=EPS),
                              reads=[("mv", s)], writes=[("rs", s)])
                        P.add("dve", lambda e: e.reciprocal(rs[s][:], rs[s][:]), reads=[("rs", s)], writes=[("rs", s)])
                        P.add("dve", lambda e: e.tensor_scalar(vn[v][:], vraw[:, 0:2048], mv[s][:, 0:1], rs[s][:],
                                                                op0=ALU.subtract, op1=ALU.mult),
                              reads=[(vk, q) for q in range(4)] + [("mv", s), ("rs", s)], writes=[("scrB", v)])
                        return v

                    def sgu(tt, v):
                        for q in range(4):
                            bk = next_bank()

                            def mm(e, q=q, bk=bk):
                                ins = None
                                for j in range(4):
                                    c = 4 * q + j
                                    ins = e.matmul(psum[bk][:, j * 128:(j + 1) * 128], vn[v][:, c * 128:(c + 1) * 128],
                                                   wsT[:, c // 2, :], start=True, stop=True)
                                return ins
                            P.add("pe", mm, reads=[("scrB", v), "wsT"], writes=[("ps", bk)])
                            ts = state["t1"] % 2
                            state["t1"] += 1
                            for j in range(4):
                                c = 4 * q + j
                                P.add("dve", lambda e, j=j, c=c, bk=bk, ts=ts: e.scalar_tensor_tensor(
                                    out=t1[ts][:, j * 128:(j + 1) * 128], in0=psum[bk][:, j * 128:(j + 1) * 128],
                                    scalar=cols[:, 16 + c:17 + c], in1=bias2[:, c, :], op0=ALU.mult, op1=ALU.add),
                                    reads=[("ps", bk), "cols", ("bias2", c)], writes=[("t1", ts, j)])
                            hk = [("H", 4 * q + j, tt) for j in range(4)]
                            P.add("pool", lambda e, q=q, ts=ts: e.tensor_tensor(
                                H[:, 4 * q:4 * q + 4, tt * 128:(tt + 1) * 128],
                                t1[ts][:].rearrange("p (c t) -> p c t", c=4),
                                H[:, 4 * q:4 * q + 4, tt * 128:(tt + 1) * 128], op=ALU.mult),
                                reads=[("t1", ts, j) for j in range(4)] + hk, writes=hk)

                    def out_proj(tt):
                        for cb, slot in ((0, os0), (1, os1)):
                            bk = gemm_T(tt, slot, 0, 16)
                            P.add("dve", lambda e, cb=cb, bk=bk: e.tensor_tensor(
                                xres[:, tt, cb * 512:(cb + 1) * 512], xres[:, tt, cb * 512:(cb + 1) * 512], psum[bk][:],
                                op=ALU.add), reads=[("ps", bk), ("xres", tt)], writes=[("xres", tt)])
                        emit_ln(h, tt, False, fast=(tt >= 6))
                        flush_T(keep=2)

                    pend = None
                    for tt in range(NT):
                        v = v_gemm(tt)
                        if pend is not None:
                            sgu(*pend)
                        pend = (tt, v)
                    issue_big(state["big_use"] + 1)
                    out_proj(0)
                    sgu(*pend)
                    for tt in range(1, NT):
                        out_proj(tt)
                else:
                    load_ln_consts(2)
                    hc = scrA[0]
                    issue_big(state["big_use"] + 3)
                    for dc in range(8):
                        banks = {}
                        sl_c = use_sm(SM_BASE["b_in"] + 8 + dc, la=3)
                        sl_h = use_sm(SM_BASE["b_in"] + 16 + dc, la=2)
                        if h == 0:
                            P.add("dve", lambda e: e.memset(hc[:, 0:2], 0.0), writes=[("hc", 0)])
                        else:
                            P.add("dve", lambda e, dc=dc: e.tensor_copy(hc[:, 0:2], convst[:, dc, :]),
                                  reads=[("convst", dc)], writes=[("hc", 0)])
                        for tb in range(2):
                            banks[("c", tb)] = gemm_F(sl_c, tb)
                            if dc == 0 and tb == 0 and li > 0:
                                flush_T(keep=1)
                            banks[("h", tb)] = gemm_F(sl_h, tb)
                            if dc == 0 and tb == 0:
                                flush_T()
                            bc_, bh_ = banks[("c", tb)], banks[("h", tb)]
                            sl = slice(tb * 512, (tb + 1) * 512)
                            P.add("act", lambda e, sl=sl, bc_=bc_: e.activation(out=scrB[0][:, sl], in_=psum[bc_][:],
                                                                                func=AF.Copy),
                                  reads=[("ps", bc_)], writes=[("csb", tb)])
                            P.add("dve", lambda e, tb=tb, sl=sl, bh_=bh_: e.tensor_tensor(
                                hc[:, 2 + tb * 512:2 + (tb + 1) * 512], psum[bh_][:], scrB[0][:, sl], op=ALU.mult),
                                reads=[("ps", bh_), ("csb", tb)], writes=[("hc", 1 + tb)])
                        sl_b = use_sm(SM_BASE["b_in"] + dc, la=3)
                        for tb in range(2):
                            banks[("b", tb)] = gemm_F(sl_b, tb)
                        if h == 0:
                            P.add("dve", lambda e, dc=dc: e.tensor_copy(convst[:, dc, :], hc[:, 1024:1026]),
                                  reads=[("hc", 2)], writes=[("convst", dc)])
                        for tb in range(2):
                            bb_ = banks[("b", tb)]
                            sl = slice(tb * 512, (tb + 1) * 512)
                            asl = slice(tb * 512, (tb + 1) * 512)
                            rdk = [("hc", 0), ("hc", 1), ("hc", 2)] if tb == 1 else [("hc", 0), ("hc", 1)]
                            P.add("act", lambda e, tb=tb, asl=asl, dc=dc: e.activation(
                                out=scrB[1][:, asl], in_=hc[:, 2 + tb * 512:2 + (tb + 1) * 512], func=AF.Identity,
                                scale=cols[:, 48 + 16 + dc:48 + 17 + dc]),
                                reads=rdk + ["cols"], writes=[("acc", tb)])
                            for k in (1, 0):
                                P.add("dve", lambda e, tb=tb, asl=asl, dc=dc, k=k: e.scalar_tensor_tensor(
                                    out=scrB[1][:, asl], in0=hc[:, k + tb * 512:k + (tb + 1) * 512],
                                    scalar=cols[:, 48 + k * 8 + dc:48 + k * 8 + dc + 1], in1=scrB[1][:, asl],
                                    op0=ALU.mult, op1=ALU.add),
                                    reads=rdk + ["cols", ("acc", tb)], writes=[("acc", tb)])
                            P.add("dve", lambda e, tb=tb, sl=sl, asl=asl, dc=dc, bb_=bb_: e.tensor_tensor(
                                H[:, dc, sl], psum[bb_][:], scrB[1][:, asl], op=ALU.mult),
                                reads=[("ps", bb_), ("acc", tb)], writes=[("H", dc, tb * 4 + i) for i in range(4)])
                    slot = use_big(BIG_BASE["b_out"])
                    for tt in range(NT):
                        for cb in range(2):
                            bk = gemm_T(tt, slot, cb * 8, 8)
                            P.add("dve", lambda e, tt=tt, cb=cb, bk=bk: e.scalar_tensor_tensor(
                                out=xres[:, tt, cb * 512:(cb + 1) * 512], in0=xres[:, tt, cb * 512:(cb + 1) * 512],
                                scalar=ALPHA, in1=psum[bk][:], op0=ALU.mult, op1=ALU.add),
                                reads=[("ps", bk), ("xres", tt)], writes=[("xres", tt)])
                        emit_ln(h, tt, False, fast=(tt >= 6), act_norm=True)
                        flush_T(keep=2)

                load_ln_consts(l * 2 + 1)
                for fh in range(2):
                    def evac_h(fc, tb, bk):
                        ts = state["t1"] % 2
                        state["t1"] += 1
                        P.add("act", lambda e: e.activation(out=t1[ts][:], in_=psum[bk][:], func=AF.Relu),
                              reads=[("ps", bk)], writes=[("t1", ts, j) for j in range(4)])
                        P.add("dve", lambda e: e.tensor_tensor(
                            H[:, fc, tb * 512:(tb + 1) * 512], t1[ts][:], t1[ts][:], op=ALU.mult),
                            reads=[("t1", ts, j) for j in range(4)],
                            writes=[("H", fc, tb * 4 + i) for i in range(4)])
                    f_phase([SM_BASE[f"w1_{l}"] + fh * 16 + fc for fc in range(16)], evac_h,
                            first_after_ln=(fh == 0))
                    s0 = use_big(BIG_BASE[f"w2_{l}"] + fh * 2 + 0)
                    s1 = use_big(BIG_BASE[f"w2_{l}"] + fh * 2 + 1, la=2)
                    for tt in range(NT):
                        for cb, slot in ((0, s0), (1, s1)):
                            bk = gemm_T(tt, slot, 0, 16)
                            if fh == 0:
                                P.add("dve", lambda e, tt=tt, cb=cb, bk=bk: e.scalar_tensor_tensor(
                                    out=xres[:, tt, cb * 512:(cb + 1) * 512], in0=xres[:, tt, cb * 512:(cb + 1) * 512],
                                    scalar=ALPHA, in1=psum[bk][:], op0=ALU.mult, op1=ALU.add),
                                    reads=[("ps", bk), ("xres", tt)], writes=[("xres", tt)])
                            else:
                                P.add("dve", lambda e, tt=tt, cb=cb, bk=bk: e.tensor_tensor(
                                    xres[:, tt, cb * 512:(cb + 1) * 512], xres[:, tt, cb * 512:(cb + 1) * 512],
                                    psum[bk][:], op=ALU.add), reads=[("ps", bk), ("xres", tt)], writes=[("xres", tt)])
                        if fh == 1:
                            emit_ln(h, tt, last_layer, fast=(tt >= 6))
                            flush_T(keep=2)
                            if last_layer and h == 0:
                                load_x_bf(1, tt, defer=True)
                                load_x_res(1, tt)
            flush_T()

        assert state["sm_use"] == len(sm_plan) and state["big_use"] == len(big_plan) and not big_pending
        P.finalize()

        streams = list(P.dma_streams.keys())
        sem_eng = {e: es.enter_context(nc.semaphore(f"s_{e}")) for e in ENGINES}
        sem_dma = {s_: es.enter_context(nc.semaphore("d_" + "_".join(str(t) for t in (s_ if isinstance(s_, tuple) else (s_,)))))
                   for s_ in streams}
        block = es.enter_context(nc.Block())

        @block.tensor
        def _(e):
            P.emit_engine("pe", e, sem_eng, sem_dma)

        @block.scalar
        def _(e):
            P.emit_engine("act", e, sem_eng, sem_dma)

        @block.vector
        def _(e):
            P.emit_engine("dve", e, sem_eng, sem_dma)

        @block.gpsimd
        def _(e):
            P.emit_engine("pool", e, sem_eng, sem_dma)

        @block.sync
        def _(e):
            P.emit_engine("sp", e, sem_eng, sem_dma)
            for s_ in streams:
                if isinstance(s_, tuple) and s_[0] == "out":
                    e.wait_ge(sem_dma[s_], 16 * P.dma_streams[s_])
    return nc


_NC_CACHE = {}


def _get_nc(layers):
    if layers not in _NC_CACHE:
        _NC_CACHE[layers] = build_nc(layers)
    return _NC_CACHE[layers]


def _run(layers, x, wts):
    nc = _get_nc(layers)
    in_maps = []
    for b in range(NB):
        m = {"x": np.ascontiguousarray(x[b])}
        m.update(wts)
        in_maps.append(m)
    res = run_bass_kernel_spmd(nc, in_maps, core_ids=list(range(NB)))
    return np.stack([np.asarray(r["out"]) for r in res.results], axis=0)


FUSED = True


def kernel(**inputs):
    x = np.asarray(inputs["x"], dtype=np.float32)
    wts = prepare_weights(inputs)
    if FUSED:
        out = _run((0, 1), x, wts)
    else:
        mid = _run((0,), x, wts)
        out = _run((1,), mid, wts)
    return out.astype(np.float32)
```

```python
from contextlib import ExitStack
import numpy as np
import concourse.bass as bass
import concourse.mybir as mybir
from concourse.bass_utils import run_bass_kernel_spmd

F32 = mybir.dt.float32
BF16 = mybir.dt.bfloat16
AF = mybir.ActivationFunctionType
ALU = mybir.AluOpType

D = 1024
SEQ = 2048
NB = 8
TH = 1024
NT = TH // 128
ALPHA = float((2.0 * 2) ** 0.25)
EPS = 1e-5
ENGINES = ("pe", "act", "dve", "pool", "sp")


ALIAS = {
    ("scrB", 0): (("B", 0, 0), ("B", 0, 1)), ("scrB", 1): (("B", 1, 0), ("B", 1, 1)),
    ("hib", 0): (("B", 0, 0),), ("hib", 1): (("B", 0, 1),),
    "hi32": (("B", 1, 0), ("B", 1, 1)),
    ("csb", 0): (("B", 0, 0),), ("csb", 1): (("B", 0, 1),),
    ("acc", 0): (("B", 1, 0),), ("acc", 1): (("B", 1, 1),),
    "b32a": (("A", 0), ("A", 1)), "b32b": (("A", 2), ("A", 3)),
    ("lo32", 0): (("A", 0), ("A", 1)), ("lo32", 1): (("A", 2), ("A", 3)),
    ("scrA0", 0): (("A", 0),), ("scrA0", 1): (("A", 1),), ("scrA0", 2): (("A", 2),), ("scrA0", 3): (("A", 3),),
    ("hc", 0): (("A", 0),), ("hc", 1): (("A", 0), ("A", 1)), ("hc", 2): (("A", 1), ("A", 2)),
}
class Op:
    __slots__ = ("eng", "fn", "reads", "writes", "is_dma", "stream", "idx", "eidx",
                 "waits", "signal", "sig_val", "dma_val")

    def __init__(self, eng, fn, reads, writes, is_dma, stream):
        self.eng = eng
        self.fn = fn
        self.reads = reads
        self.writes = writes
        self.is_dma = is_dma
        self.stream = stream
        self.waits = []
        self.signal = False
        self.sig_val = None
        self.dma_val = None


class Prog:
    def __init__(self):
        self.ops = []
        self.eng_ops = {e: [] for e in ENGINES}
        self.last_writer = {}
        self.readers = {}
        self.dma_streams = {}
        self.waited = {e: {} for e in ENGINES}
        self.bank_busy = {}

    def _expand(self, keys):
        out = []
        for k in keys:
            out.extend(ALIAS.get(k, (k,)))
        return tuple(out)

    def add(self, eng, fn, reads=(), writes=(), is_dma=False, stream=None):
        op = Op(eng, fn, self._expand(reads), self._expand(writes), is_dma, stream)
        for k in op.writes:
            if isinstance(k, tuple) and k[0] == "ps" and eng == "pe":
                assert not self.bank_busy.get(k[1], False), ("PSUM bank reused before its evacuation was emitted", k)
                self.bank_busy[k[1]] = True
        for k in op.reads:
            if isinstance(k, tuple) and k[0] == "ps" and eng != "pe":
                self.bank_busy[k[1]] = False
        op.idx = len(self.ops)
        op.eidx = len(self.eng_ops[eng])
        self.ops.append(op)
        self.eng_ops[eng].append(op)
        if is_dma:
            c = self.dma_streams.get(stream, 0) + 1
            self.dma_streams[stream] = c
            op.dma_val = 16 * c
        deps = {}

        def need(src, raw):
            if src is None or src is op:
                return
            prev = deps.get(id(src))
            deps[id(src)] = (src, raw if prev is None else (prev[1] or raw))

        for k in op.reads:
            need(self.last_writer.get(k), True)
        for k in op.writes:
            need(self.last_writer.get(k), False)
            rd = self.readers.get(k)
            if rd:
                for r in rd.values():
                    need(r, False)
        best = {}
        for src, raw in deps.values():
            if src.is_dma:
                key = ("dma", src.stream)
                if src.dma_val > self.waited[eng].get(key, 0):
                    self.waited[eng][key] = src.dma_val
                    op.waits.append(("dma", src.stream, src.dma_val))
                continue
            if src.eng == eng and not op.is_dma:
                if eng == "pe" or not raw:
                    continue
            cur = best.get(src.eng)
            if cur is None or src.eidx > cur.eidx:
                best[src.eng] = src
        for se, src in best.items():
            key = ("eng", se)
            if src.eidx > self.waited[eng].get(key, -1):
                self.waited[eng][key] = src.eidx
                src.signal = True
                op.waits.append(("eng", src, None))
        for k in op.writes:
            self.last_writer[k] = op
            self.readers[k] = {}
        for k in op.reads:
            d = self.readers.setdefault(k, {})
            d[eng if not is_dma else ("dma", op.idx)] = op
        return op

    def finalize(self):
        for e in ENGINES:
            c = 0
            for op in self.eng_ops[e]:
                if op.signal:
                    assert not op.is_dma
                    c += 1
                    op.sig_val = c

    def emit_engine(self, name, eng, sem_eng, sem_dma):
        for op in self.eng_ops[name]:
            for kind, a, b in op.waits:
                if kind == "dma":
                    eng.wait_ge(sem_dma[a], b)
                else:
                    eng.wait_ge(sem_eng[a.eng], a.sig_val)
            ins = op.fn(eng)
            if op.is_dma:
                ins.then_inc(sem_dma[op.stream], 16)
            elif op.signal:
                ins.then_inc(sem_eng[name], 1)


def _fblocks(w):
    K, N = w.shape
    nf = N // 128
    return np.ascontiguousarray(w.reshape(K // 128, 128, nf, 128).transpose(2, 1, 0, 3).reshape(nf, 128, (K // 128) * 128))


def _tblock(w, r0, nr, c0):
    nk = nr // 128
    return w[r0:r0 + nr, c0:c0 + 512].reshape(nk, 128, 512).transpose(1, 0, 2).reshape(128, nk * 512)


def prepare_weights(inp):
    f = lambda a: np.asarray(a, dtype=np.float32)
    a_w_in = f(inp["a_w_in"])[0]
    a_w_out = f(inp["a_w_out"])[0]
    b_w_in = f(inp["b_w_in"])[0]
    b_w_out = f(inp["b_w_out"])[0]
    w1 = f(inp["mlp_w1"])
    w2 = f(inp["mlp_w2"])
    wsm = np.concatenate([_fblocks(a_w_in[:, :2048]), _fblocks(b_w_in), _fblocks(w1[0]), _fblocks(w1[1])], axis=0)
    big = []
    for i in range(2):
        big.append(np.concatenate([_tblock(a_w_in, 0, 1024, 2048 + (2 * i) * 512),
                                   _tblock(a_w_in, 0, 1024, 2048 + (2 * i + 1) * 512)], axis=1))
    for cb in range(2):
        big.append(_tblock(a_w_out, 0, 2048, cb * 512))
    for l in range(2):
        for fh in range(2):
            for cb in range(2):
                big.append(_tblock(w2[l], fh * 2048, 2048, cb * 512))
    big.append(np.concatenate([_tblock(b_w_out, 0, 1024, 0), _tblock(b_w_out, 0, 1024, 512)], axis=1))
    wbig = np.ascontiguousarray(np.stack(big, axis=0))
    cols = np.zeros((128, 73), np.float32)
    cols[0, 72] = 1.0
    b_in = f(inp["a_b_in"])[0]
    cols[:, 0:16] = b_in[:2048].reshape(16, 128).T
    cols[:, 16:32] = f(inp["a_v_g"])[0].reshape(16, 128).T
    cols[:, 32:48] = f(inp["a_v_b"])[0].reshape(16, 128).T
    conv = f(inp["b_conv"])[0]
    for k in range(3):
        cols[:, 48 + k * 8:48 + (k + 1) * 8] = conv[k].reshape(8, 128).T
    rows = np.zeros((12, 1024), np.float32)
    ln_g = f(inp["ln_g"]); ln_b = f(inp["ln_b"])
    for l in range(2):
        for s in range(2):
            rows[l * 2 + s] = ln_g[l, s]
            rows[4 + l * 2 + s] = ln_b[l, s]
    rows[8] = f(inp["a_b_out"])[0]
    rows[9:11] = b_in[2048:].reshape(2, 1024)
    rows[11] = f(inp["a_b_s"])[0].reshape(1024)
    wsT = np.ascontiguousarray(f(inp["a_w_s"])[0].transpose(2, 0, 1).reshape(128, 1024))
    return {"wsm": wsm, "wbig": wbig, "cols": cols, "rows": rows, "wsT": wsT}


SM_BASE = {"a_u": 0, "b_in": 16, "w1_0": 40, "w1_1": 72}
BIG_BASE = {"a_v": 0, "a_out": 2, "w2_0": 4, "w2_1": 8, "b_out": 12}


NCOLS = 73


def build_nc(layers=(0, 1)):
    nc = bass.Bass("TRN2", target_bir_lowering=False)
    x_d = nc.dram_tensor("x", [SEQ, D], F32, kind="ExternalInput")
    wsm_d = nc.dram_tensor("wsm", [104, 128, 1024], F32, kind="ExternalInput")
    wbig_d = nc.dram_tensor("wbig", [13, 128, 8192], F32, kind="ExternalInput")
    cols_d = nc.dram_tensor("cols", [128, NCOLS], F32, kind="ExternalInput")
    rows_d = nc.dram_tensor("rows", [12, 1024], F32, kind="ExternalInput")
    wsT_d = nc.dram_tensor("wsT", [128, 1024], F32, kind="ExternalInput")
    out_d = nc.dram_tensor("out", [SEQ, D], F32, kind="ExternalOutput")
    x_ap = x_d.ap()
    out_ap = out_d.ap()
    wsm_ap = wsm_d.ap()
    wbig_ap = wbig_d.ap()

    def row_bc(r0, n, parts=128):
        return bass.AP(rows_d, r0, [[0, parts], [1, n]])

    es = ExitStack()
    with es:
        def sb(name, shape, dt):
            return es.enter_context(nc.sbuf_tensor(name, shape, dt))

        xres = sb("xres", [128, NT, D], F32)
        xT = sb("xT", [128, 8, TH], BF16)
        H = sb("H", [128, 16, TH], BF16)
        wbig = [sb(f"wbig{i}", [128, 16, 512], BF16) for i in range(4)]
        wsm = [sb(f"wsm{i}", [128, 8, 128], BF16) for i in range(4)]
        scrA = [sb(f"scrA{i}", [128, 2050], F32) for i in range(1)]
        scrB = [sb(f"scrB{i}", [128, 1024], F32) for i in range(2)]
        t1 = [sb(f"t1_{i}", [128, 512], F32) for i in range(2)]
        xb = [sb(f"xb{i}", [128, D], BF16) for i in range(3)]
        lnc = sb("lnc", [128, 2, D], F32)
        bias2 = sb("bias2", [128, 16, 128], F32)
        bhl = sb("bhl", [2, 2048], BF16)
        wsT = sb("wsT_sb", [128, 8, 128], BF16)
        ones_bf = sb("ones_bf", [128, 128], BF16)
        ident = sb("ident", [128, 128], BF16)
        cols = sb("cols_sb", [128, NCOLS], F32)
        convst = sb("convst", [128, 8, 2], F32)
        st = [sb(f"st{i}", [128, 4, 6], F32) for i in range(2)]
        mv = [sb(f"mv{i}", [128, 2], F32) for i in range(2)]
        rs = [sb(f"rs{i}", [128, 1], F32) for i in range(2)]
        nm = [sb(f"nm{i}", [128, 1], F32) for i in range(2)]
        psum = [es.enter_context(nc.psum_tensor(f"ps{i}", [128, 512], F32)) for i in range(8)]
        psum_bf = [p.bitcast(BF16) for p in psum]
        vn = [s_.bitcast(BF16) for s_ in scrB]

        P = Prog()
        state = {"bank": 0, "sm_issue": 0, "sm_use": 0, "big_issue": 0, "big_use": 0, "t1": 0,
                 "xb": 0, "stat": 0, "vslot": 0, "ost": 0}
        big_pending = []
        pend_T = []

        def next_bank():
            b = state["bank"] % 8
            state["bank"] += 1
            return b

        sm_plan, big_plan = [], []
        for h in range(2):
            for l in layers:
                if l == 0:
                    sm_plan += [SM_BASE["a_u"] + i for i in range(16)]
                    big_plan += [BIG_BASE["a_v"], BIG_BASE["a_v"] + 1, BIG_BASE["a_out"], BIG_BASE["a_out"] + 1]
                else:
                    for dc in range(8):
                        sm_plan += [SM_BASE["b_in"] + 8 + dc, SM_BASE["b_in"] + 16 + dc, SM_BASE["b_in"] + dc]
                    big_plan += [BIG_BASE["b_out"]]
                for fh in range(2):
                    sm_plan += [SM_BASE[f"w1_{l}"] + fh * 16 + i for i in range(16)]
                    big_plan += [BIG_BASE[f"w2_{l}"] + fh * 2 + cb for cb in range(2)]

        def pump_big(n=1, upto_block=None):
            while big_pending and (n > 0 or (upto_block is not None and big_pending[0][0] <= upto_block)):
                i, slot, q, blk = big_pending.pop(0)
                P.add("pool", lambda e, slot=slot, q=q, blk=blk: e.dma_start(
                    out=wbig[slot][:, 4 * q:4 * q + 4, :].rearrange("p a b -> p (a b)"),
                    in_=wbig_ap[blk][:, q * 2048:(q + 1) * 2048]),
                    writes=[("wb", slot, q)], is_dma=True, stream=("wb", slot, q))
                n -= 1

        def issue_sm(upto):
            while state["sm_issue"] <= upto and state["sm_issue"] < len(sm_plan):
                i = state["sm_issue"]
                slot = i % 4
                blk = sm_plan[i]
                P.add("pool", lambda e, slot=slot, blk=blk: e.dma_start(
                    out=wsm[slot][:].rearrange("p a b -> p (a b)"), in_=wsm_ap[blk]),
                    writes=[("ws", slot)], is_dma=True, stream=("ws", slot))
                state["sm_issue"] += 1
                pump_big(1)

        def issue_big(upto):
            while state["big_issue"] <= upto and state["big_issue"] < len(big_plan):
                i = state["big_issue"]
                for q in range(4):
                    big_pending.append((i, i % 4, q, big_plan[i]))
                state["big_issue"] += 1

        def use_sm(blk, la=3):
            i = state["sm_use"]
            assert sm_plan[i] == blk, (i, sm_plan[i], blk)
            issue_sm(i + la)
            state["sm_use"] += 1
            return i % 4

        def use_big(blk, la=3):
            i = state["big_use"]
            assert big_plan[i] == blk, (i, big_plan[i], blk)
            issue_big(i + la)
            pump_big(0, upto_block=i)
            state["big_use"] += 1
            return i % 4

        P.add("sp", lambda e: e.dma_start(out=cols[:], in_=cols_d.ap()), writes=["cols"], is_dma=True, stream="c0")
        identf = scrB[0]
        P.add("dve", lambda e: e.memset(identf[:, 0:128], 1.0), writes=[("scrB", 0)])
        P.add("pool", lambda e: e.affine_select(out=identf[:, 0:128], in_=identf[:, 0:128], pattern=[[-1, 128]],
                                                  compare_op=ALU.is_equal, fill=0.0, base=0, channel_multiplier=1),
              reads=[("scrB", 0)], writes=[("scrB", 0)])
        P.add("dve", lambda e: e.tensor_copy(ident[:], identf[:, 0:128]), reads=[("scrB", 0)], writes=["ident"])
        def init_bias2():
            for g in range(8):
                bk = next_bank()
                P.add("pe", lambda e, g=g, bk=bk: e.matmul(psum[bk][:, 0:128], ones_bf[:], wsT[:, g, :],
                                                           start=True, stop=True),
                      reads=["ones", "wsT"], writes=[("ps", bk)])
                for j in range(2):
                    c = 2 * g + j
                    P.add("dve", lambda e, g=g, c=c, bk=bk: e.scalar_tensor_tensor(
                        out=bias2[:, c, :], in0=psum[bk][:, 0:128], scalar=cols[:, 32 + c:33 + c],
                        in1=lnc[:, 1, g * 128:(g + 1) * 128], op0=ALU.mult, op1=ALU.add),
                        reads=[("ps", bk), "cols", "lnc_b"], writes=[("bias2", c)])

        def init_layer0():
            P.add("dve", lambda e: e.memset(ones_bf[:], 1.0), writes=["ones"])
            P.add("pool", lambda e: e.dma_start(out=wsT[:].rearrange("p a b -> p (a b)"), in_=wsT_d.ap()),
                  writes=["wsT"], is_dma=True, stream="c1")
            P.add("dve", lambda e: e.memset(wsT[64:128, :, 0:64], 0.0), reads=["wsT"], writes=["wsT"])
            bsbc = lnc[:, 1, :]
            P.add("sp", lambda e: e.dma_start(out=lnc[:, 1, :], in_=row_bc(11 * 1024, 1024)), writes=["lnc_b"],
                  is_dma=True, stream="lncb")
            b32 = scrA[0]
            P.add("sp", lambda e: e.dma_start(out=b32[0:2, 0:1024], in_=row_bc(9 * 1024, 1024, parts=2)),
                  writes=["b32a"], is_dma=True, stream="c3")
            P.add("sp", lambda e: e.dma_start(out=b32[0:2, 1024:2048], in_=row_bc(10 * 1024, 1024, parts=2)),
                  writes=["b32b"], is_dma=True, stream="c5")
            for hf in range(2):
                sl = slice(hf * 1024, (hf + 1) * 1024)
                src = b32[0:2, hf * 1024:(hf + 1) * 1024]
                P.add("dve", lambda e, sl=sl, src=src: e.tensor_copy(vn[0][0:2, sl], src),
                      reads=["b32a", "b32b", "ident"], writes=[("hib", hf), ("scrB", 0)])
                P.add("dve", lambda e, sl=sl: e.tensor_copy(scrB[1][0:2, 0:1024], vn[0][0:2, sl]),
                      reads=[("hib", hf)], writes=["hi32", ("scrB", 1)])
                P.add("dve", lambda e, src=src: e.tensor_tensor(src, src, scrB[1][0:2, 0:1024], op=ALU.subtract),
                      reads=["hi32", "b32a", "b32b"], writes=[("lo32", hf)])
                P.add("dve", lambda e, src=src: e.tensor_tensor(scrB[1][0:2, 0:1024], scrB[1][0:2, 0:1024], src,
                                                                 op=ALU.subtract),
                      reads=["hi32", ("lo32", hf)], writes=["hi32"])
                P.add("dve", lambda e, sl=sl, src=src: e.scalar_tensor_tensor(
                    out=bhl[0:2, sl], in0=scrB[1][0:2, 0:1024], scalar=cols[0:2, 72:73], in1=src,
                    op0=ALU.mult, op1=ALU.add),
                    reads=["hi32", ("lo32", hf), "cols"], writes=["bhl"])
        if 0 in layers:
            init_layer0()

        def load_ln_consts(idx):
            P.add("sp", lambda e: e.dma_start(out=lnc[:, 0, :], in_=row_bc(idx * 1024, 1024)),
                  writes=["lnc_g"], is_dma=True, stream="lncg")
            P.add("sp", lambda e: e.dma_start(out=lnc[:, 1, :], in_=row_bc((4 + idx) * 1024, 1024)),
                  writes=["lnc_b"], is_dma=True, stream="lncb")

        def transpose_to_xT(xs, tt):
            bks = (next_bank(), next_bank())
            src_bf = xb[xs]

            def tr(e):
                ins = None
                for c in range(8):
                    ins = e.matmul(psum[bks[c // 4]][:, (c % 4) * 128:(c % 4 + 1) * 128],
                                   src_bf[:, c * 128:(c + 1) * 128], ident[:], start=True, stop=True)
                return ins
            P.add("pe", tr, reads=[("xb", xs), "ident"], writes=[("ps", bks[0]), ("ps", bks[1])])
            for q in range(2):
                P.add("act", lambda e, q=q: e.activation(
                    out=xT[:, 4 * q:4 * q + 4, tt * 128:(tt + 1) * 128],
                    in_=psum[bks[q]][:].rearrange("p (c t) -> p c t", c=4), func=AF.Copy),
                    reads=[("ps", bks[q])], writes=[("xT", tt, q)])

        def flush_T(keep=0):
            while len(pend_T) > keep:
                xs, tt = pend_T.pop(0)
                transpose_to_xT(xs, tt)

        def emit_ln(h, tt, final, fast=False, act_norm=False):
            z = xres[:, tt, :]
            zk = ("xres", tt)
            s = state["stat"] % 2
            state["stat"] += 1
            P.add("dve", lambda e: e.bn_stats(st[s][:, 0, :], z[:, 0:512]), reads=[zk], writes=[("st", s, 0)])
            P.add("dve", lambda e: e.bn_stats(st[s][:, 1, :], z[:, 512:1024]), reads=[zk], writes=[("st", s, 1)])
            P.add("dve", lambda e: e.bn_aggr(mv[s][:], st[s][:, 0:2, :]), reads=[("st", s, 0), ("st", s, 1)],
                  writes=[("mv", s)])
            P.add("act", lambda e: e.activation(out=rs[s][:], in_=mv[s][:, 1:2], func=AF.Sqrt, bias=EPS),
                  reads=[("mv", s)], writes=[("rs", s)])
            P.add("dve", lambda e: e.reciprocal(rs[s][:], rs[s][:]), reads=[("rs", s)], writes=[("rs", s)])
            if act_norm:
                P.add("dve", lambda e: e.scalar_tensor_tensor(out=nm[s][:], in0=mv[s][:, 0:1], scalar=-1.0, in1=rs[s][:],
                                                               op0=ALU.mult, op1=ALU.mult),
                      reads=[("mv", s), ("rs", s)], writes=[("nm", s)])
                P.add("act", lambda e: e.activation(out=z, in_=z, func=AF.Identity, scale=rs[s][:], bias=nm[s][:]),
                      reads=[zk, ("nm", s), ("rs", s)], writes=[zk])
            else:
                P.add("dve", lambda e: e.tensor_scalar(z, z, mv[s][:, 0:1], rs[s][:], op0=ALU.subtract, op1=ALU.mult),
                      reads=[zk, ("mv", s), ("rs", s)], writes=[zk])
            P.add("dve", lambda e: e.tensor_tensor(z, z, lnc[:, 0, :], op=ALU.mult), reads=[zk, "lnc_g"], writes=[zk])
            if not final:
                P.add("dve" if fast else "pool", lambda e: e.tensor_tensor(z, z, lnc[:, 1, :], op=ALU.add),
                      reads=[zk, "lnc_b"], writes=[zk])
                xs = state["xb"] % 3
                state["xb"] += 1
                P.add("act", lambda e: e.activation(out=xb[xs][:], in_=z, func=AF.Copy), reads=[zk],
                      writes=[("xb", xs)])
                pend_T.append((xs, tt))
            else:
                o = state["ost"] % 2
                state["ost"] += 1
                P.add("dve" if fast else "pool", lambda e: e.tensor_tensor(scrB[o][:], z, lnc[:, 1, :], op=ALU.add),
                      reads=[zk, "lnc_b"], writes=[("scrB", o)])
                r0 = (h * NT + tt) * 128
                P.add("sp", lambda e: e.dma_start(out=out_ap[r0:r0 + 128, :], in_=scrB[o][:]),
                      reads=[("scrB", o)], is_dma=True, stream=("out", o))

        def gemm_F(slot, tb, nk=8):
            bk = next_bank()

            def mm(e):
                ins = None
                for kc in range(nk):
                    ins = e.matmul(psum[bk][:], wsm[slot][:, kc, :], xT[:, kc, tb * 512:(tb + 1) * 512],
                                   start=(kc == 0), stop=(kc == nk - 1))
                return ins
            P.add("pe", mm, reads=[("ws", slot)] + [("xT", tb * 4 + i, q) for i in range(4) for q in range(2)],
                  writes=[("ps", bk)])
            return bk

        def gemm_T(tt, slot, ent0, nk, hchunks=True, bias_cb=None):
            bk = next_bank()
            A = H if hchunks else xT

            def mm(e):
                ins = None
                for kc in range(nk):
                    ins = e.matmul(psum[bk][:], A[:, kc, tt * 128:(tt + 1) * 128], wbig[slot][:, ent0 + kc, :],
                                   start=(kc == 0), stop=(kc == nk - 1 and bias_cb is None))
                if bias_cb is not None:
                    ins = e.matmul(psum[bk][:], ones_bf[0:2, :], bhl[0:2, bias_cb * 512:(bias_cb + 1) * 512],
                                   start=False, stop=True)
                return ins
            rd = [("wb", slot, q) for q in range(ent0 // 4, (ent0 + nk - 1) // 4 + 1)]
            rd += ([("H", kc, tt) for kc in range(nk)] if hchunks else [("xT", tt, 0), ("xT", tt, 1)])
            if bias_cb is not None:
                rd += ["bhl", "ones"]
            P.add("pe", mm, reads=rd, writes=[("ps", bk)])
            pump_big(1)
            return bk

        def f_phase(blocks, evac, first_after_ln):
            n = len(blocks)
            issue_big(state["big_use"] + 3)
            for p0 in range(0, n, 2):
                pair = list(range(p0, min(p0 + 2, n)))
                slots = {}
                for k, i in enumerate(pair):
                    slots[i] = use_sm(blocks[i], la=3 - k)
                for tb in range(2):
                    for i in pair:
                        bk = gemm_F(slots[i], tb)
                        evac(i, tb, bk)
                        if first_after_ln and p0 == 0 and tb == 0:
                            flush_T(keep=(1 if i == pair[0] and len(pair) > 1 else 0))
            flush_T()

        def load_x_bf(h, tt, defer=False):
            r0 = (h * NT + tt) * 128
            xs = state["xb"] % 3
            state["xb"] += 1
            P.add("pool", lambda e: e.dma_start(out=xb[xs][:], in_=x_ap[r0:r0 + 128, :]),
                  writes=[("xb", xs)], is_dma=True, stream=("xbl", xs))
            if defer:
                pend_T.append((xs, tt))
            else:
                transpose_to_xT(xs, tt)

        def load_x_res(h, tt):
            r0 = (h * NT + tt) * 128
            P.add("sp", lambda e: e.dma_start(out=xres[:, tt, :], in_=x_ap[r0:r0 + 128, :]),
                  writes=[("xres", tt)], is_dma=True, stream=("xr", tt))

        for h in range(2):
            if h == 0:
                for tt in range(NT):
                    load_x_bf(h, tt)
                    load_x_res(h, tt)
                if 0 in layers:
                    init_bias2()

            for li, l in enumerate(layers):
                last_layer = (li == len(layers) - 1)
                if l == 0:
                    load_ln_consts(0)
                    P.add("sp", lambda e: e.dma_start(out=scrB[1][:], in_=row_bc(8 * 1024, 1024)),
                          writes=[("scrB", 1)], is_dma=True, stream="c2")
                    for tt in range(NT):
                        P.add("dve", lambda e, tt=tt: e.scalar_tensor_tensor(
                            out=xres[:, tt, :], in0=xres[:, tt, :], scalar=ALPHA, in1=scrB[1][:],
                            op0=ALU.mult, op1=ALU.add),
                            reads=[("xres", tt), ("scrB", 1)], writes=[("xres", tt)])

                    def evac_u(fc, tb, bk):
                        P.add("act", lambda e: e.activation(
                            out=H[:, fc, tb * 512:(tb + 1) * 512], in_=psum[bk][:], func=AF.Gelu,
                            bias=cols[:, fc:fc + 1]),
                            reads=[("ps", bk), "cols"], writes=[("H", fc, tb * 4 + i) for i in range(4)])
                    f_phase([SM_BASE["a_u"] + fc for fc in range(16)], evac_u, first_after_ln=(li > 0))
                    vs0 = use_big(BIG_BASE["a_v"], la=3)
                    vs1 = use_big(BIG_BASE["a_v"] + 1, la=2)
                    os0 = use_big(BIG_BASE["a_out"], la=1)
                    os1 = use_big(BIG_BASE["a_out"] + 1, la=0)
                    vslots = (vs0, vs1)

                    def v_gemm(tt):
                        v = state["vslot"] % 2
                        state["vslot"] += 1
                        vraw = scrA[0]
                        vk = "scrA0"
                        for cb in range(4):
                            bk = gemm_T(tt, vslots[cb // 2], (cb % 2) * 8, 8, hchunks=False, bias_cb=cb)
                            P.add("act", lambda e, cb=cb, bk=bk: e.activation(
                                out=vraw[:, cb * 512:(cb + 1) * 512], in_=psum[bk][:], func=AF.Gelu),
                                reads=[("ps", bk)], writes=[(vk, cb)])
                        s = state["stat"] % 2
                        state["stat"] += 1
                        for q in range(4):
                            P.add("dve", lambda e, q=q: e.bn_stats(st[s][:, q, :], vraw[:, q * 512:(q + 1) * 512]),
                                  reads=[(vk, q)], writes=[("st", s, q)])
                        P.add("dve", lambda e: e.bn_aggr(mv[s][:], st[s][:]), reads=[("st", s, q) for q in range(4)],
                              writes=[("mv", s)])
                        P.add("act", lambda e: e.activation(out=rs[s][:], in_=mv[s][:, 1:2], func=AF.Sqrt, bias=EPS),
                              reads=[("mv", s)], writes=[("rs", s)])
                        P.add("dve", lambda e: e.reciprocal(rs[s][:], rs[s][:]), reads=[("rs", s)], writes=[("rs", s)])
                        P.add("dve", lambda e: e.tensor_scalar(vn[v][:], vraw[:, 0:2048], mv[s][:, 0:1], rs[s][:],
                                                                op0=ALU.subtract, op1=ALU.mult),
                              reads=[(vk, q) for q in range(4)] + [("mv", s), ("rs", s)], writes=[("scrB", v)])
                        return v

                    def sgu(tt, v):
                        for q in range(4):
                            bk = next_bank()

                            def mm(e, q=q, bk=bk):
                                ins = None
                                for j in range(4):
                                    c = 4 * q + j
                                    ins = e.matmul(psum[bk][:, j * 128:(j + 1) * 128], vn[v][:, c * 128:(c + 1) * 128],
                                                   wsT[:, c // 2, :], start=True, stop=True)
                                return ins
                            P.add("pe", mm, reads=[("scrB", v), "wsT"], writes=[("ps", bk)])
                            ts = state["t1"] % 2
                            state["t1"] += 1
                            for j in range(4):
                                c = 4 * q + j
                                P.add("dve", lambda e, j=j, c=c, bk=bk, ts=ts: e.scalar_tensor_tensor(
                                    out=t1[ts][:, j * 128:(j + 1) * 128], in0=psum[bk][:, j * 128:(j + 1) * 128],
                                    scalar=cols[:, 16 + c:17 + c], in1=bias2[:, c, :], op0=ALU.mult, op1=ALU.add),
                                    reads=[("ps", bk), "cols", ("bias2", c)], writes=[("t1", ts, j)])
                            hk = [("H", 4 * q + j, tt) for j in range(4)]
                            P.add("pool", lambda e, q=q, ts=ts: e.tensor_tensor(
                                H[:, 4 * q:4 * q + 4, tt * 128:(tt + 1) * 128],
                                t1[ts][:].rearrange("p (c t) -> p c t", c=4),
                                H[:, 4 * q:4 * q + 4, tt * 128:(tt + 1) * 128], op=ALU.mult),
                                reads=[("t1", ts, j) for j in range(4)] + hk, writes=hk)

                    def out_proj(tt):
                        for cb, slot in ((0, os0), (1, os1)):
                            bk = gemm_T(tt, slot, 0, 16)
                            P.add("dve", lambda e, cb=cb, bk=bk: e.tensor_tensor(
                                xres[:, tt, cb * 512:(cb + 1) * 512], xres[:, tt, cb * 512:(cb + 1) * 512], psum[bk][:],
                                op=ALU.add), reads=[("ps", bk), ("xres", tt)], writes=[("xres", tt)])
                        emit_ln(h, tt, False, fast=(tt >= 6))
                        flush_T(keep=2)

                    pend = None
                    for tt in range(NT):
                        v = v_gemm(tt)
                        if pend is not None:
                            sgu(*pend)
                        pend = (tt, v)
                    out_proj(0)
                    sgu(*pend)
                    for tt in range(1, NT):
                        out_proj(tt)
                else:
                    load_ln_consts(2)
                    hc = scrA[0]
                    issue_big(state["big_use"] + 3)
                    for dc in range(8):
                        banks = {}
                        sl_c = use_sm(SM_BASE["b_in"] + 8 + dc, la=3)
                        sl_h = use_sm(SM_BASE["b_in"] + 16 + dc, la=2)
                        if h == 0:
                            P.add("dve", lambda e: e.memset(hc[:, 0:2], 0.0), writes=[("hc", 0)])
                        else:
                            P.add("dve", lambda e, dc=dc: e.tensor_copy(hc[:, 0:2], convst[:, dc, :]),
                                  reads=[("convst", dc)], writes=[("hc", 0)])
                        for tb in range(2):
                            banks[("c", tb)] = gemm_F(sl_c, tb)
                            if dc == 0 and tb == 0 and li > 0:
                                flush_T(keep=1)
                            banks[("h", tb)] = gemm_F(sl_h, tb)
                            if dc == 0 and tb == 0:
                                flush_T()
                            bc_, bh_ = banks[("c", tb)], banks[("h", tb)]
                            sl = slice(tb * 512, (tb + 1) * 512)
                            P.add("act", lambda e, sl=sl, bc_=bc_: e.activation(out=scrB[0][:, sl], in_=psum[bc_][:],
                                                                                func=AF.Copy),
                                  reads=[("ps", bc_)], writes=[("csb", tb)])
                            P.add("dve", lambda e, tb=tb, sl=sl, bh_=bh_: e.tensor_tensor(
                                hc[:, 2 + tb * 512:2 + (tb + 1) * 512], psum[bh_][:], scrB[0][:, sl], op=ALU.mult),
                                reads=[("ps", bh_), ("csb", tb)], writes=[("hc", 1 + tb)])
                        sl_b = use_sm(SM_BASE["b_in"] + dc, la=3)
                        for tb in range(2):
                            banks[("b", tb)] = gemm_F(sl_b, tb)
                        if h == 0:
                            P.add("dve", lambda e, dc=dc: e.tensor_copy(convst[:, dc, :], hc[:, 1024:1026]),
                                  reads=[("hc", 2)], writes=[("convst", dc)])
                        for tb in range(2):
                            bb_ = banks[("b", tb)]
                            sl = slice(tb * 512, (tb + 1) * 512)
                            asl = slice(tb * 512, (tb + 1) * 512)
                            rdk = [("hc", 0), ("hc", 1), ("hc", 2)] if tb == 1 else [("hc", 0), ("hc", 1)]
                            P.add("act", lambda e, tb=tb, asl=asl, dc=dc: e.activation(
                                out=scrB[1][:, asl], in_=hc[:, 2 + tb * 512:2 + (tb + 1) * 512], func=AF.Identity,
                                scale=cols[:, 48 + 16 + dc:48 + 17 + dc]),
                                reads=rdk + ["cols"], writes=[("acc", tb)])
                            for k in (1, 0):
                                P.add("dve", lambda e, tb=tb, asl=asl, dc=dc, k=k: e.scalar_tensor_tensor(
                                    out=scrB[1][:, asl], in0=hc[:, k + tb * 512:k + (tb + 1) * 512],
                                    scalar=cols[:, 48 + k * 8 + dc:48 + k * 8 + dc + 1], in1=scrB[1][:, asl],
                                    op0=ALU.mult, op1=ALU.add),
                                    reads=rdk + ["cols", ("acc", tb)], writes=[("acc", tb)])
                            P.add("dve", lambda e, tb=tb, sl=sl, asl=asl, dc=dc, bb_=bb_: e.tensor_tensor(
                                H[:, dc, sl], psum[bb_][:], scrB[1][:, asl], op=ALU.mult),
                                reads=[("ps", bb_), ("acc", tb)], writes=[("H", dc, tb * 4 + i) for i in range(4)])
                    slot = use_big(BIG_BASE["b_out"])
                    for tt in range(NT):
                        for cb in range(2):
                            bk = gemm_T(tt, slot, cb * 8, 8)
                            P.add("dve", lambda e, tt=tt, cb=cb, bk=bk: e.scalar_tensor_tensor(
                                out=xres[:, tt, cb * 512:(cb + 1) * 512], in0=xres[:, tt, cb * 512:(cb + 1) * 512],
                                scalar=ALPHA, in1=psum[bk][:], op0=ALU.mult, op1=ALU.add),
                                reads=[("ps", bk), ("xres", tt)], writes=[("xres", tt)])
                        emit_ln(h, tt, False, fast=(tt >= 6), act_norm=True)
                        flush_T(keep=2)

                load_ln_consts(l * 2 + 1)
                for fh in range(2):
                    def evac_h(fc, tb, bk):
                        ts = state["t1"] % 2
                        state["t1"] += 1
                        P.add("act", lambda e: e.activation(out=t1[ts][:], in_=psum[bk][:], func=AF.Relu),
                              reads=[("ps", bk)], writes=[("t1", ts, j) for j in range(4)])
                        P.add("dve", lambda e: e.tensor_tensor(
                            H[:, fc, tb * 512:(tb + 1) * 512], t1[ts][:], t1[ts][:], op=ALU.mult),
                            reads=[("t1", ts, j) for j in range(4)],
                            writes=[("H", fc, tb * 4 + i) for i in range(4)])
                    f_phase([SM_BASE[f"w1_{l}"] + fh * 16 + fc for fc in range(16)], evac_h,
                            first_after_ln=(fh == 0))
                    s0 = use_big(BIG_BASE[f"w2_{l}"] + fh * 2 + 0)
                    s1 = use_big(BIG_BASE[f"w2_{l}"] + fh * 2 + 1, la=2)
                    for tt in range(NT):
                        for cb, slot in ((0, s0), (1, s1)):
                            bk = gemm_T(tt, slot, 0, 16)
                            if fh == 0:
                                P.add("dve", lambda e, tt=tt, cb=cb, bk=bk: e.scalar_tensor_tensor(
                                    out=xres[:, tt, cb * 512:(cb + 1) * 512], in0=xres[:, tt, cb * 512:(cb + 1) * 512],
                                    scalar=ALPHA, in1=psum[bk][:], op0=ALU.mult, op1=ALU.add),
                                    reads=[("ps", bk), ("xres", tt)], writes=[("xres", tt)])
                            else:
                                P.add("dve", lambda e, tt=tt, cb=cb, bk=bk: e.tensor_tensor(
                                    xres[:, tt, cb * 512:(cb + 1) * 512], xres[:, tt, cb * 512:(cb + 1) * 512],
                                    psum[bk][:], op=ALU.add), reads=[("ps", bk), ("xres", tt)], writes=[("xres", tt)])
                        if fh == 1:
                            emit_ln(h, tt, last_layer, fast=(tt >= 6))
                            flush_T(keep=2)
                            if last_layer and h == 0:
                                load_x_bf(1, tt, defer=True)
                                load_x_res(1, tt)
            flush_T()

        assert state["sm_use"] == len(sm_plan) and state["big_use"] == len(big_plan) and not big_pending
        P.finalize()

        streams = list(P.dma_streams.keys())
        sem_eng = {e: es.enter_context(nc.semaphore(f"s_{e}")) for e in ENGINES}
        sem_dma = {s_: es.enter_context(nc.semaphore("d_" + "_".join(str(t) for t in (s_ if isinstance(s_, tuple) else (s_,)))))
                   for s_ in streams}
        block = es.enter_context(nc.Block())

        @block.tensor
        def _(e):
            P.emit_engine("pe", e, sem_eng, sem_dma)

        @block.scalar
        def _(e):
            P.emit_engine("act", e, sem_eng, sem_dma)

        @block.vector
        def _(e):
            P.emit_engine("dve", e, sem_eng, sem_dma)

        @block.gpsimd
        def _(e):
            P.emit_engine("pool", e, sem_eng, sem_dma)

        @block.sync
        def _(e):
            P.emit_engine("sp", e, sem_eng, sem_dma)
            for s_ in streams:
                if isinstance(s_, tuple) and s_[0] == "out":
                    e.wait_ge(sem_dma[s_], 16 * P.dma_streams[s_])
    return nc


_NC_CACHE = {}


def _get_nc(layers):
    if layers not in _NC_CACHE:
        _NC_CACHE[layers] = build_nc(layers)
    return _NC_CACHE[layers]


def _run(layers, x, wts):
    nc = _get_nc(layers)
    in_maps = []
    for b in range(NB):
        m = {"x": np.ascontiguousarray(x[b])}
        m.update(wts)
        in_maps.append(m)
    res = run_bass_kernel_spmd(nc, in_maps, core_ids=list(range(NB)))
    return np.stack([np.asarray(r["out"]) for r in res.results], axis=0)


FUSED = True


def kernel(**inputs):
    x = np.asarray(inputs["x"], dtype=np.float32)
    wts = prepare_weights(inputs)
    if FUSED:
        out = _run((0, 1), x, wts)
    else:
        mid = _run((0,), x, wts)
        out = _run((1,), mid, wts)
    return out.astype(np.float32)
```
